# Optimizing a Trainium2 kernel written in Bass

```python
import jax, jax.numpy as jnp
from jax import lax
import numpy as np

D_MODEL = 1024
BATCH = 2
SEQ = 8192
DEPTH = 1
DEC_BATCH = 16
DEC_SEQ = 4096
PAST_LEN = 128

GRID_W = 64
N_MEM = 256
EPS = 1e-6
POOL_GROUPS = 4
POOL_GROUP_DIM = D_MODEL // 8
POOL_WIDTH = POOL_GROUPS * POOL_GROUP_DIM
POOL_WINDOWS = (2, 4, 8, 16)
N_HEADS = 8
N_KV_HEADS = 2
HEAD_DIM = 64
ATTN_WIDTH = N_HEADS * HEAD_DIM
KV_WIDTH = N_KV_HEADS * HEAD_DIM
AXIS_DIM = HEAD_DIM // 2
ROPE_THETA = 10000.0
Q_BLOCK = 128
N_X_HEADS = 4
X_HEAD_DIM = D_MODEL // 8
X_WIDTH = N_X_HEADS * X_HEAD_DIM
N_BRANCH = 3
BRANCH_WIDTH = 512
IN_WIDTHS = (POOL_WIDTH, POOL_WIDTH, ATTN_WIDTH, KV_WIDTH, KV_WIDTH, ATTN_WIDTH, X_WIDTH, X_WIDTH, N_BRANCH * D_MODEL)
IN_DIM = 2 * POOL_WIDTH + 2 * ATTN_WIDTH + 2 * KV_WIDTH + 2 * X_WIDTH + N_BRANCH * D_MODEL

kernel_name = "hybrid_pool_gqa_xattn_gated_encoder"


def rms_norm(x, g):
    xf = x.astype(jnp.float32)
    y = xf * lax.rsqrt(jnp.mean(xf * xf, axis=-1, keepdims=True) + EPS)
    return (y * g.astype(jnp.float32)).astype(x.dtype)


def axial_rope_tables(L):
    rows = L // GRID_W
    row = jnp.repeat(jnp.arange(rows, dtype=jnp.float32), GRID_W)
    col = jnp.tile(jnp.arange(GRID_W, dtype=jnp.float32), rows)
    inv = ROPE_THETA ** (-jnp.arange(0, AXIS_DIM, 2, dtype=jnp.float32) / AXIS_DIM)
    ang = jnp.concatenate([row[:, None] * inv, col[:, None] * inv], axis=-1)
    return jnp.cos(ang), jnp.sin(ang)


def apply_rope(x, cos, sin):
    B, L, H, D = x.shape
    xf = x.astype(jnp.float32).reshape(B, L, H, D // 2, 2)
    x0, x1 = xf[..., 0], xf[..., 1]
    c = cos[None, :, None, :]
    s = sin[None, :, None, :]
    out = jnp.stack([x0 * c - x1 * s, x0 * s + x1 * c], axis=-1)
    return out.reshape(B, L, H, D).astype(x.dtype)


def multiscale_pool(u, w_pool, pool_scale):
    B, L, _ = u.shape
    uf = u.astype(jnp.float32).reshape(B, L, POOL_GROUPS, POOL_GROUP_DIM)
    cs = jnp.concatenate([jnp.zeros((B, 1, POOL_GROUPS, POOL_GROUP_DIM), jnp.float32),
                          jnp.cumsum(uf, axis=1)], axis=1)
    t = jnp.arange(L, dtype=jnp.int32)
    pooled = []
    for g, w in enumerate(POOL_WINDOWS):
        lo = jnp.clip(t - w // 2, 0, L)
        hi = jnp.clip(t + (w - 1 - w // 2) + 1, 0, L)
        csg = cs[:, :, g]
        ssum = jnp.take(csg, hi, axis=1) - jnp.take(csg, lo, axis=1)
        cnt = (hi - lo).astype(jnp.float32)
        pooled.append(ssum / cnt[None, :, None])
    pooled = jnp.stack(pooled, axis=2)
    mixed = (pooled - uf).astype(u.dtype)
    out = jnp.einsum('blgc,gcd->blgd', mixed, w_pool).reshape(B, L, POOL_WIDTH)
    return out * pool_scale


def self_attention(q, k, v):
    B, L = q.shape[:2]
    G = N_HEADS // N_KV_HEADS
    nb = L // Q_BLOCK
    qb = q.reshape(B, nb, Q_BLOCK, N_KV_HEADS, G, HEAD_DIM).transpose(1, 0, 2, 3, 4, 5)
    scale = HEAD_DIM ** -0.5

    def one_block(qblk):
        s = jnp.einsum('bqkgd,bskd->bkgqs', qblk, k).astype(jnp.float32) * scale
        p = jax.nn.softmax(s, axis=-1).astype(v.dtype)
        return jnp.einsum('bkgqs,bskd->bqkgd', p, v)

    o = lax.map(one_block, qb)
    return o.transpose(1, 0, 2, 3, 4, 5).reshape(B, L, ATTN_WIDTH)


def cross_attention(xq, mem_n, w_mem_kv):
    B, L, _ = xq.shape
    M = mem_n.shape[1]
    kv = mem_n @ w_mem_kv
    mk = kv[..., :X_WIDTH].reshape(B, M, N_X_HEADS, X_HEAD_DIM)
    mv = kv[..., X_WIDTH:].reshape(B, M, N_X_HEADS, X_HEAD_DIM)
    q = xq.reshape(B, L, N_X_HEADS, X_HEAD_DIM)
    s = jnp.einsum('blhd,bmhd->bhlm', q, mk).astype(jnp.float32) * (X_HEAD_DIM ** -0.5)
    p = jax.nn.softmax(s, axis=-1).astype(mv.dtype)
    return jnp.einsum('bhlm,bmhd->blhd', p, mv).reshape(B, L, X_WIDTH)


def encoder_layer(x, mem, ln_pre, ln_post, ln_mem, w_in, b_merge, q_norm, k_norm,
                  w_pool, pool_scale, w_mem_kv, w_branch, w_out):
    B, L, _ = x.shape
    h = rms_norm(x, ln_pre)
    z = h @ w_in
    split_at = np.cumsum(IN_WIDTHS)[:-1].tolist()
    (pool_in, pool_gate, q, k, v, attn_gate, xq, x_gate, merge_logits) = jnp.split(z, split_at, axis=-1)

    pool_out = multiscale_pool(pool_in, w_pool, pool_scale)

    cos, sin = axial_rope_tables(L)
    q = apply_rope(rms_norm(q.reshape(B, L, N_HEADS, HEAD_DIM), q_norm), cos, sin)
    k = apply_rope(rms_norm(k.reshape(B, L, N_KV_HEADS, HEAD_DIM), k_norm), cos, sin)
    attn_out = self_attention(q, k, v.reshape(B, L, N_KV_HEADS, HEAD_DIM))

    cross_out = cross_attention(xq, rms_norm(mem, ln_mem), w_mem_kv)

    branches = (pool_out * jax.nn.silu(pool_gate),
                attn_out * jax.nn.silu(attn_gate),
                cross_out * jax.nn.silu(x_gate))
    gates = jax.nn.sigmoid(merge_logits.reshape(B, L, N_BRANCH, D_MODEL) + b_merge)
    merged = jnp.zeros_like(x)
    for n in range(N_BRANCH):
        merged = merged + gates[:, :, n] * (branches[n] @ w_branch[n])

    y = rms_norm(merged @ w_out, ln_post)
    return x + y


def setup_inputs(seed: int = 0) -> dict:
    key = jax.random.key(seed)
    ks = jax.random.split(key, 20)
    f32 = jnp.float32
    nrm = lambda k, shape, s: (jax.random.normal(k, shape, f32) * s).astype(f32)
    return {
        "x_prompt": nrm(ks[0], (BATCH, SEQ, D_MODEL), 1.0),
        "x_sample": nrm(ks[1], (DEC_BATCH, DEC_SEQ, D_MODEL), 1.0),
        "mem_prompt": nrm(ks[2], (BATCH, N_MEM, D_MODEL), 1.0),
        "mem_sample": nrm(ks[3], (DEC_BATCH, N_MEM, D_MODEL), 1.0),
        "ln_pre": 1.0 + nrm(ks[4], (D_MODEL,), 0.02),
        "ln_post": 1.0 + nrm(ks[5], (D_MODEL,), 0.02),
        "ln_mem": 1.0 + nrm(ks[6], (D_MODEL,), 0.02),
        "w_in": nrm(ks[7], (D_MODEL, IN_DIM), D_MODEL ** -0.5),
        "b_merge": nrm(ks[8], (N_BRANCH, D_MODEL), 0.01),
        "q_norm": 1.0 + nrm(ks[9], (HEAD_DIM,), 0.02),
        "k_norm": 1.0 + nrm(ks[10], (HEAD_DIM,), 0.02),
        "w_pool": nrm(ks[11], (POOL_GROUPS, POOL_GROUP_DIM, POOL_GROUP_DIM), POOL_GROUP_DIM ** -0.5),
        "pool_scale": 1.0 + nrm(ks[12], (POOL_WIDTH,), 0.1),
        "w_mem_kv": nrm(ks[13], (D_MODEL, 2 * X_WIDTH), D_MODEL ** -0.5),
        "w_branch": nrm(ks[14], (N_BRANCH, BRANCH_WIDTH, D_MODEL), BRANCH_WIDTH ** -0.5),
        "w_out": nrm(ks[15], (D_MODEL, D_MODEL), D_MODEL ** -0.5),
    }


def reference(x_prompt, x_sample, mem_prompt, mem_sample, ln_pre, ln_post, ln_mem, w_in, b_merge,
              q_norm, k_norm, w_pool, pool_scale, w_mem_kv, w_branch, w_out):
    y_prompt = x_prompt
    y_sample = x_sample
    for _ in range(DEPTH):
        y_prompt = encoder_layer(y_prompt, mem_prompt, ln_pre, ln_post, ln_mem, w_in, b_merge, q_norm, k_norm,
                                 w_pool, pool_scale, w_mem_kv, w_branch, w_out)
        y_sample = encoder_layer(y_sample, mem_sample, ln_pre, ln_post, ln_mem, w_in, b_merge, q_norm, k_norm,
                                 w_pool, pool_scale, w_mem_kv, w_branch, w_out)
    return (y_prompt, y_sample)
```

```python
from contextlib import ExitStack
import numpy as np
import ml_dtypes
import concourse.bass as bass
import concourse.mybir as mybir
from concourse.bass_utils import run_bass_kernel_spmd

F32 = mybir.dt.float32
BF16 = mybir.dt.bfloat16
ACT = mybir.ActivationFunctionType
ALU = mybir.AluOpType
AX = mybir.AxisListType

D = 1024
EPS = 1e-6
NMEM = 256
TQ = 512
POOL_W = (2, 4, 8, 16)
N_CORES = 8


class Buf:
    __slots__ = ("name", "w", "r")

    def __init__(self, name):
        self.name = name
        self.w = None
        self.r = []


class Prog:
    ENG = ("pe", "act", "dve", "pool", "sp")

    def __init__(self, nc):
        self.nc = nc
        self.h = {"pe": nc.tensor, "act": nc.scalar, "dve": nc.vector, "pool": nc.gpsimd, "sp": nc.sync}
        self.ops = {k: [] for k in self.ENG}
        self.ctx = ExitStack()
        self.sems = {}
        self.cnt = {}
        self.waited = {k: {} for k in self.ENG}
        self.pending = {k: False for k in self.ENG}
        for k in self.ENG:
            self.new_sem("e_" + k)
        self.nbuf = 0
        self.n_instr = 0

    def new_sem(self, name):
        self.sems[name] = self.ctx.enter_context(self.nc.semaphore(name))
        self.cnt[name] = 0
        return name

    def buf(self, name=None):
        self.nbuf += 1
        return Buf(name or f"b{self.nbuf}")

    def sb(self, name, shape, dtype):
        return self.ctx.enter_context(self.nc.sbuf_tensor("sb_" + name, list(shape), dtype))

    def _need(self, eng, reads, writes, join_sem=None):
        need = {}

        def add(pt, raw):
            if pt is None:
                return
            sem, val, peng = pt
            if peng is not None and peng == eng and eng == "pe":
                return
            if val > need.get(sem, 0):
                need[sem] = val

        for b in reads:
            add(b.w, True)
        for b in writes:
            if not (join_sem is not None and b.w is not None and b.w[0] == join_sem and b.w[2] is None):
                add(b.w, False)
            for p in b.r:
                add(p, False)
        return need

    def _emit_waits(self, eng, need):
        h = self.h[eng]
        for sem, val in need.items():
            if val > self.waited[eng].get(sem, 0):
                self.waited[eng][sem] = val
                h.wait_ge(self.sems[sem], val)

    def op(self, eng, fn, reads=(), writes=(), signal=True):
        self.n_instr += 1
        need = self._need(eng, reads, writes)
        self._emit_waits(eng, need)
        sem = "e_" + eng
        if signal:
            self.cnt[sem] += 1
            val = self.cnt[sem]
            fn().then_inc(self.sems[sem], 1)
            self.pending[eng] = False
        else:
            val = self.cnt[sem] + 1
            fn()
            self.pending[eng] = True
        pt = (sem, val, eng)
        for b in reads:
            b.r.append(pt)
            if len(b.r) > 64:
                b.r = _compact(b.r)
        for b in writes:
            b.w = pt
            b.r = []

    def dma(self, queue, out, in_, sem, reads=(), writes=(), join=False):
        self.n_instr += 1
        need = self._need(None, reads, writes, join_sem=(sem if join else None))
        self._emit_waits(queue, need)
        self.cnt[sem] += 16
        val = self.cnt[sem]
        self.h[queue].dma_start(out=out, in_=in_).then_inc(self.sems[sem], 16)
        pt = (sem, val, None)
        for b in reads:
            b.r.append(pt)
            if len(b.r) > 64:
                b.r = _compact(b.r)
        for b in writes:
            b.w = pt
            b.r = []

    def inherit(self, dst, src):
        pts = []
        for s in src:
            if s.w is not None:
                pts.append(s.w)
            pts.extend(s.r)
        pts = _compact(pts)
        for d in dst:
            d.r = _compact(list(d.r) + pts)

    def wait_all(self, eng, bufs):
        need = {}
        for b in bufs:
            for p in ([b.w] if b.w else []) + list(b.r):
                if p[1] > need.get(p[0], 0):
                    need[p[0]] = p[1]
        self._emit_waits(eng, need)

    def finish(self):
        for e in self.ENG:
            assert not self.pending[e], f"engine {e} has unsignalled trailing instruction"
        self.ctx.close()


def _compact(pts):
    best = {}
    for p in pts:
        k = (p[0], p[2])
        if k not in best or p[1] > best[k][1]:
            best[k] = p
    return list(best.values())


U_POOL_IN, U_POOL_GATE, U_Q, U_ATTN_GATE, U_XQ, U_XGATE = 0, 1, 2, 3, 4, 5
U_MERGE0 = 6
U_BR0 = 12
U_OUT0 = 18
U_MK, U_MV = 20, 21
N_UNITS = 22


import os as _os
_PROBE = {k: True for k in _os.environ.get("KPROBE", "").split(",") if k}


class _Stop(Exception):
    pass


def build(cfg):
    NS, LS, LKP, LQP = cfg["NS"], cfg["LS"], cfg["LKP"], cfg["LQP"]

    def ckpt(k):
        if cfg.get("stop") == k:
            raise _Stop()

    LKMAX = max(LS, LKP)
    nc = bass.Bass("TRN2", target_bir_lowering=False)
    P = Prog(nc)

    def din(name, shape, dt=F32):
        return nc.dram_tensor(name, list(shape), dt, kind="ExternalInput").ap()

    xs_d = din("xs", [NS, LS + 16, D])
    xp_d = din("xp", [LKP, D])
    xqp_d = din("xqp", [LQP + 16, D])
    mem_d = din("mem", [NS + 1, NMEM, D])
    rope_s_d = din("rope_s", [LS, 64])
    rope_p_d = din("rope_p", [LKP, 64])
    rope_qp_d = din("rope_qp", [LQP, 64])
    edge_s_d = din("edge_s", [LS // TQ, 128, 64])
    edge_qp_d = din("edge_qp", [LQP // TQ, 128, 64])
    w_in_d = din("w_in", [D, 6400])
    w_mkv_d = din("w_mem_kv", [D, 1024])
    w_br_d = din("w_branch", [3, 512, D])
    w_out_d = din("w_out", [D, D])
    w_pool_d = din("w_pool", [4, 128, 128])
    lnpre_d = din("lnpre_l", [128, 8])
    lnmem_d = din("lnmem_l", [128, 8])
    bmerge_d = din("bmerge_l", [128, 24])
    lnpost_d = din("lnpost_b", [128, D])
    geo_d = din("geo", [128, 128])
    pscale_d = din("pscale_b", [128, 512])
    ident_d = din("ident", [128, 128])
    ys_d = nc.dram_tensor("ys", [NS, LS, D], F32, kind="ExternalOutput").ap()
    yp_d = nc.dram_tensor("yp", [LQP, D], F32, kind="ExternalOutput").ap()
    scr_d = nc.dram_tensor("wscr", [N_UNITS, 128, 4096], BF16, kind="Internal").ap()
    scr_b = [P.buf(f"scr{u}") for u in range(N_UNITS)]

    ps = P.ctx.enter_context(nc.psum_tensor("ps", [128, 4096], F32))
    bank_b = [P.buf(f"bank{i}") for i in range(8)]

    def bank(i, n=1):
        return ps[:, i * 512:(i + n) * 512]

    def bank_bf(i):
        return ps[:, i * 512:(i + 1) * 512].bitcast(BF16)

    rr = [0]

    def next_bank():
        b = rr[0] % 8
        rr[0] += 1
        return b

    def next_pair():
        if rr[0] % 2:
            rr[0] += 1
        b = rr[0] % 8
        rr[0] += 2
        return b

    def next_quad(avoid=None):
        while rr[0] % 4:
            rr[0] += 1
        if avoid is not None and rr[0] % 8 == avoid:
            rr[0] += 4
        b = rr[0] % 8
        rr[0] += 4
        return b

    def T(name, shape, dt):
        return P.sb(name, shape, dt), P.buf(name)

    ident_f, ident_f_b = T("ident_f", [128, 128], F32)
    ident, ident_b = T("ident", [128, 128], BF16)
    lnpre, lnpre_b = T("lnpre", [128, 8], F32)
    lnmem, lnmem_b = T("lnmem", [128, 8], F32)
    bmh, bmh_b = T("bmh", [128, 24], F32)
    lnpost, lnpost_b = T("lnpost", [128, D], F32)
    geo, geo_b = T("geo", [128, 128], F32)
    pscale, pscale_b = T("pscale", [128, 512], F32)
    negM, negM_b = T("negM", [128, 1], F32)
    mxa, mxa_b = T("mxa", [128, 2], F32)
    mhalf, mhalf_b = T("mhalf", [128, 32], F32)
    Wkv, Wkv_b = T("Wkv", [128, 8, 256], BF16)
    Wpool, Wpool_b = T("Wpool", [128, 4, 128], BF16)
    KT, KT_b = T("KT", [128, LKMAX], BF16)
    VA, VA_b = T("VA", [128, LKMAX // 128, 192], BF16)
    memKT, memKT_b = T("memKT", [128, 4, NMEM], BF16)
    memV, memV_b = T("memV", [128, 2, 512], BF16)
    NRING = 3
    ring = [T(f"wr{i}", [128, 4096], BF16) for i in range(NRING)]
    NXT = 4
    xt = [T(f"xt{i}", [128, D], F32) for i in range(NXT)]
    xbf = [T(f"xbf{i}", [128, D], BF16) for i in range(2)]
    hTs = [T(f"hT{i}", [128, 8, TQ], BF16) for i in range(2)]
    hThs = [T(f"hTh{i}", [128, 8, 16], BF16) for i in range(2)]
    ropes = [T(f"rope_t{i}", [128, 4, 64], F32) for i in range(2)]
    tabss = [T(f"tabs{i}", [128, 4, 4, 32], F32) for i in range(2)]
    edges = [T(f"edge_t{i}", [128, 64], F32) for i in range(2)]
    stxs = [[T(f"stx{j}_{i}", [128, 4], F32) for i in range(3)] for j in range(2)]
    sth = [T(f"sth_{i}", [16, 1], F32) for i in range(3)]
    uT, uT_b = T("uT", [128, 4, TQ + 16], F32)
    pa, pa_b = T("pa", [128, TQ + 16], F32)
    pb, pb_b = T("pb", [128, TQ + 16], F32)
    mixT, mixT_b = T("mixT", [128, 4, TQ], BF16)
    ovA, sqr_b = T("ovA", [128, 4 * TQ], F32)
    ovB, xqn_b = T("ovB", [128, 4 * TQ], F32)
    macc = [(ovA[:, c * TQ:(c + 1) * TQ], P.buf(f"macc{c}")) for c in range(4)]
    ytmps = [(uT[:].rearrange("p g t -> p (g t)")[:, k * D:(k + 1) * D], P.buf(f"ytmp{k}")) for k in range(2)]
    mT = ovB[:].bitcast(BF16).rearrange("p (k t) -> p k t", k=8)
    mT_b = P.buf("mT")
    qbf, qbf_b = T("qbf", [128, 4 * TQ], BF16)
    st32 = [T(f"st32_{i}", [128, 32], F32) for i in range(3)]
    QT, QT_b = T("QT", [128, 4, TQ], BF16)
    xqT, xqT_b = T("xqT", [128, 4, TQ], BF16)
    brall = P.sb("brall", [128, 3 * D], F32)
    br = [(brall[:, n * D:(n + 1) * D].bitcast(BF16).rearrange("p (c t) -> p c t", c=4), None) for n in range(3)]
    br_cb = [[P.buf(f"br{n}_{c}") for c in range(4)] for n in range(3)]
    xtx = [(brall[:, n * D:(n + 1) * D], P.buf(f"xtx{n}")) for n in range(3)]
    NPT = 3
    PT = [T(f"PT{i}", [128, 2 * TQ], BF16) for i in range(NPT)]
    rc, rc_b = T("rc", [128, TQ], F32)
    gp, gp_b = T("gp", [128, TQ], F32)
    gtmp = [T(f"gtmp{i}", [128, TQ], F32) for i in range(2)]
    ptmp, ptmp_b = T("ptmp", [128, TQ], F32)
    pexp, pexp_b = T("pexp", [128, 8, NMEM], BF16)
    pTt, pTt_b = T("pTt", [128, 16, 128], BF16)
    xst = [T(f"xst{i}", [128, 8], F32) for i in range(4)]

    for s in (["d_ring%d" % i for i in range(NRING)] + ["d_xt%d" % i for i in range(NXT + 3)] +
              ["d_rope0", "d_rope1", "d_edge0", "d_edge1"] + ["d_c%d" % i for i in range(9)]):
        P.new_sem(s)

    def v_ts(out, in0, s1, s2, op0, op1=None, reads=(), writes=(), eng="dve"):
        h = nc.vector if eng == "dve" else nc.gpsimd
        if op1 is None:
            P.op(eng, lambda: h.tensor_scalar(out=out, in0=in0, scalar1=s1, scalar2=None, op0=op0), reads, writes)
        else:
            P.op(eng, lambda: h.tensor_scalar(out=out, in0=in0, scalar1=s1, scalar2=s2, op0=op0, op1=op1), reads, writes)

    def v_tt(out, in0, in1, op, reads=(), writes=(), eng="dve"):
        h = nc.vector if eng == "dve" else nc.gpsimd
        P.op(eng, lambda: h.tensor_tensor(out=out, in0=in0, in1=in1, op=op), reads, writes)

    def v_stt(out, in0, scalar, in1, op0, op1, reads=(), writes=()):
        P.op("dve", lambda: nc.vector.scalar_tensor_tensor(out=out, in0=in0, scalar=scalar, in1=in1, op0=op0, op1=op1),
             reads, writes)

    def v_copy(out, in_, reads=(), writes=(), eng="dve"):
        if eng == "dve":
            P.op("dve", lambda: nc.vector.tensor_copy(out=out, in_=in_), reads, writes)
        elif eng == "act":
            P.op("act", lambda: nc.scalar.copy(out=out, in_=in_), reads, writes)
        else:
            P.op("pool", lambda: nc.gpsimd.tensor_copy(out=out, in_=in_), reads, writes)

    def mm(out, lhsT, rhs, start, stop, reads=(), writes=(), signal=None):
        if signal is None:
            signal = stop
        P.op("pe", lambda: nc.tensor.matmul(out, lhsT=lhsT, rhs=rhs, start=start, stop=stop), reads, writes, signal=signal)

    def tr(out, in_, reads=(), writes=(), signal=True):
        P.op("pe", lambda: nc.tensor.transpose(out=out, in_=in_, identity=ident[0:in_.shape[0], 0:in_.shape[0]]),
             list(reads) + [ident_b], writes, signal=signal)

    def rsqrt_small(ss_ap, v_ap, r_ap, mh_ap, scale, eps, ss_b, v_b, r_b):
        v_ts(v_ap, ss_ap, scale, eps, ALU.mult, ALU.add, reads=[ss_b], writes=[v_b], eng="pool")
        P.op("pool", lambda: nc.gpsimd.tensor_tensor(out=r_ap, in0=v_ap, in1=mh_ap, op=ALU.pow),
             reads=[v_b, mhalf_b], writes=[r_b])

    ring_state = {"n": 0}

    def ring_load(unit, ncols=4096):
        i = ring_state["n"] % NRING
        ring_state["n"] += 1
        t, b = ring[i]
        P.dma("sp", t[:, 0:ncols], scr_d[unit][:, 0:ncols], f"d_ring{i}", reads=[scr_b[unit]], writes=[b])
        return t, b

    cidx = [0]

    def cload(t_ap, d_ap, b):
        s = f"d_c{cidx[0] % 9}"
        cidx[0] += 1
        P.dma("sp", t_ap, d_ap, s, writes=[b])

    cload(ident_f[:], ident_d, ident_f_b)
    cload(lnpre[:], lnpre_d, lnpre_b)
    cload(lnmem[:], lnmem_d, lnmem_b)
    cload(bmh[:], bmerge_d, bmh_b)
    cload(lnpost[:], lnpost_d, lnpost_b)
    cload(geo[:], geo_d, geo_b)
    cload(pscale[:], pscale_d, pscale_b)
    v_copy(ident[:], ident_f[:], reads=[ident_f_b], writes=[ident_b])
    P.op("pool", lambda: nc.gpsimd.memset(mhalf[:], -0.5), writes=[mhalf_b])
    P.op("pool", lambda: nc.gpsimd.memset(VA[:, :, 64:128], 1.0), writes=[VA_b])
    v_ts(bmh[:], bmh[:], 0.5, None, ALU.mult, reads=[bmh_b], writes=[bmh_b])
    P.op("dve", lambda: nc.vector.tensor_reduce(out=mxa[:, 0:1], in_=geo[:, 0:64], axis=AX.X, op=ALU.max,
                                                apply_absolute_value=True), reads=[geo_b], writes=[mxa_b])
    P.op("dve", lambda: nc.vector.tensor_reduce(out=mxa[:, 1:2], in_=geo[:, 64:128], axis=AX.X, op=ALU.max,
                                                apply_absolute_value=True), reads=[geo_b, mxa_b], writes=[mxa_b])
    v_stt(negM[:], mxa[:, 0:1], -8.0, mxa[:, 1:2], ALU.mult, ALU.mult, reads=[mxa_b], writes=[negM_b])

    stage_i = [0]

    def stage_load(loads):
        i = stage_i[0] % NXT
        stage_i[0] += 1
        t, b = xt[i]
        for k, (lo, hi, plo, phi, d_ap) in enumerate(loads):
            P.dma("sp", t[plo:phi, lo:hi], d_ap, f"d_xt{i}", writes=[b], join=(k > 0))
        return t, b, i

    cv = [0]

    def convert_unit(unit, pieces, ncols_total, dest=None):
        if dest is None:
            ri = ring_state["n"] % NRING
            ring_state["n"] += 1
            rt, rb = ring[ri]
        else:
            rt, rb = dest
        for (loads, convs) in pieces:
            t, b, _ = stage_load(loads)
            for (lo, hi, sc, dlo) in convs:
                dst = rt[:, dlo:dlo + (hi - lo)] if dest is None else rt[dlo]
                eng = "dve" if cv[0] % 2 == 0 else "act"
                cv[0] += 1
                rd = [b] + ([lnpre_b, lnmem_b] if sc is not None else [])
                if sc is None:
                    v_copy(dst, t[:, lo:hi], reads=rd, writes=[rb], eng=eng)
                elif eng == "dve":
                    v_ts(dst, t[:, lo:hi], sc, None, ALU.mult, reads=rd, writes=[rb])
                else:
                    P.op("act", lambda dst=dst, src=t[:, lo:hi], sc=sc: nc.scalar.activation(out=dst, in_=src, func=ACT.Identity, scale=sc),
                         reads=rd, writes=[rb])
        if dest is None:
            P.dma("act", scr_d[unit][:, 0:ncols_total], rt[:, 0:ncols_total], f"d_ring{ri}", reads=[rb], writes=[scr_b[unit]])

    def win_cols(unit):
        if unit == U_POOL_IN:
            return [(0 + c * 128, 128) for c in range(4)]
        if unit == U_POOL_GATE:
            return [(512 + c * 128, 128) for c in range(4)]
        if unit == U_Q:
            return [[(1024 + b * 64, 64), (1024 + (b + 4) * 64, 64)] for b in range(4)]
        if unit == U_ATTN_GATE:
            return [[(1792 + b * 64, 64), (1792 + (b + 4) * 64, 64)] for b in range(4)]
        if unit == U_XQ:
            return [(2304 + c * 128, 128) for c in range(4)]
        if unit == U_XGATE:
            return [(2816 + c * 128, 128) for c in range(4)]
        n, half = divmod(unit - U_MERGE0, 2)
        return [(3328 + n * 1024 + half * 512 + c * 128, 128) for c in range(4)]

    w_in_v = w_in_d.rearrange("(kc p) c -> p kc c", p=128)
    w_mkv_v = w_mkv_d.rearrange("(kc p) c -> p kc c", p=128)
    w_out_v = w_out_d.rearrange("(kc p) c -> p kc c", p=128)

    def conv_win(unit):
        cols = win_cols(unit)
        pieces = []
        for kp in range(4):
            loads, convs = [], []
            for j in range(2):
                kc = kp * 2 + j
                if isinstance(cols[0], list):
                    for c in range(4):
                        off = 0
                        for (c0, ln) in cols[c]:
                            loads.append((j * 512 + c * 128 + off, j * 512 + c * 128 + off + ln, 0, 128, w_in_v[:, kc, c0:c0 + ln]))
                            off += ln
                else:
                    loads.append((j * 512, j * 512 + 512, 0, 128, w_in_v[:, kc, cols[0][0]:cols[0][0] + 512]))
                convs.append((j * 512, j * 512 + 512, lnpre[:, kc:kc + 1], kc * 512))
            pieces.append((loads, convs))
        convert_unit(unit, pieces, 4096)

    def conv_plain(unit, view, half, scale_t):
        pieces = []
        for kp in range(4):
            loads = [(j * 512, j * 512 + 512, 0, 128, view[:, kp * 2 + j, half * 512:half * 512 + 512]) for j in range(2)]
            convs = [(j * 512, j * 512 + 512, (scale_t[:, kp * 2 + j:kp * 2 + j + 1] if scale_t is not None else None),
                      (kp * 2 + j) * 512) for j in range(2)]
            pieces.append((loads, convs))
        convert_unit(unit, pieces, 4096)

    pieces = []
    for kp in range(2):
        loads = [(j * 256, j * 256 + 256, 0, 128, w_in_v[:, kp * 4 + j, 1536:1792]) for j in range(4)]
        convs = [(j * 256, j * 256 + 256, lnpre[:, kp * 4 + j:kp * 4 + j + 1], kp * 4 + j) for j in range(4)]
        pieces.append((loads, convs))
    convert_unit(None, pieces, 0, dest=([Wkv[:, kc, :] for kc in range(8)], Wkv_b))
    for half in range(2):
        conv_plain(U_MK + half, w_mkv_v, half, lnmem)
    for unit in range(0, 12):
        conv_win(unit)
    for n in range(3):
        for half in range(2):
            pieces = []
            for kp in range(2):
                loads, convs = [], []
                for j in range(2):
                    kc = kp * 2 + j
                    if n == 1:
                        loads.append((j * 512, j * 512 + 512, 0, 64, w_br_d[n, kc * 64:kc * 64 + 64, half * 512:half * 512 + 512]))
                        loads.append((j * 512, j * 512 + 512, 64, 128, w_br_d[n, (kc + 4) * 64:(kc + 4) * 64 + 64, half * 512:half * 512 + 512]))
                    else:
                        loads.append((j * 512, j * 512 + 512, 0, 128, w_br_d[n, kc * 128:kc * 128 + 128, half * 512:half * 512 + 512]))
                    convs.append((j * 512, j * 512 + 512, None, kc * 512))
                pieces.append((loads, convs))
            convert_unit(U_BR0 + n * 2 + half, pieces, 2048)
    for half in range(2):
        conv_plain(U_OUT0 + half, w_out_v, half, None)
    t, b, si = stage_load([])
    for g in range(4):
        P.dma("sp", t[:, g * 128:(g + 1) * 128], w_pool_d[g], f"d_xt{si}", writes=[b], join=(g > 0))
    v_tt(Wpool[:].rearrange("p g d -> p (g d)"), t[:, 0:512], pscale[:], ALU.mult, reads=[b, pscale_b], writes=[Wpool_b])

    prep_n = [0]
    xt_n = [0]

    def next_xt(kv=False):
        n_ = NXT + 3 if kv else NXT
        i = xt_n[0] % n_
        xt_n[0] += 1
        if i < NXT:
            return xt[i][0][:], xt[i][1], i
        return xtx[i - NXT][0], xtx[i - NXT][1], i

    class Prep:
        def __init__(self, x_rows_ap, nsub, rope_ap=None, gsel=None, edge_ap=None, halo_aps=None, kv=False):
            self.kv = kv
            self.sl = prep_n[0] % 2
            prep_n[0] += 1
            self.x, self.nsub, self.rope_ap, self.gsel, self.edge_ap, self.halo_aps = x_rows_ap, nsub, rope_ap, gsel, edge_ap, halo_aps
            self.xl = {}
            self.xh = None
            self.bki = 0

        def _bank(self, banks):
            if banks is None:
                return next_bank()
            b_ = banks[self.bki % len(banks)]
            self.bki += 1
            return b_

        def start(self):
            self.loads()
            self.squares()

        def loads(self):
            sl = self.sl
            if self.rope_ap is not None:
                rt_, rt_b = ropes[sl]
                P.dma("sp", rt_[:], self.rope_ap.rearrange("(s p) c -> p s c", p=128), f"d_rope{sl}", writes=[rt_b])
            if self.edge_ap is not None:
                P.dma("sp", edges[sl][0][:], self.edge_ap, f"d_edge{sl}", writes=[edges[sl][1]])
            for s in range(self.nsub):
                t, b, i = next_xt(self.kv)
                P.dma("sp", t, self.x[s * 128:(s + 1) * 128, :], f"d_xt{i}", writes=[b])
                self.xl[s] = (t, b)

        def squares(self):
            sl = self.sl
            (ss, ss_b), (vv, vv_b), (rr_, rr_b) = stxs[sl]
            hT, hT_b = hTs[sl]
            for s in range(self.nsub):
                t, b = self.xl[s]
                P.op("act", lambda t=t, s=s: nc.scalar.activation(out=hT[:, :, s * 128:(s + 1) * 128], in_=t.rearrange("p (k c) -> p k c", k=8),
                                                                  func=ACT.Square, accum_out=ss[:, s:s + 1]),
                     reads=[b], writes=[hT_b, ss_b])
            if self.kv:
                ns = self.nsub
                rsqrt_small(ss[:, 0:ns], vv[:, 0:ns], rr_[:, 0:ns], mhalf[:, 0:ns], 1.0 / D, EPS, ss_b, vv_b, rr_b)
            else:
                for s in range(self.nsub):
                    rsqrt_small(ss[:, s:s + 1], vv[:, s:s + 1], rr_[:, s:s + 1], mhalf[:, 0:1], 1.0 / D, EPS, ss_b, vv_b, rr_b)

        def scale(self, subs):
            (ss, ss_b), (vv, vv_b), (rr_, rr_b) = stxs[self.sl]
            for s in subs:
                t, b = self.xl[s]
                xb, xb_b = xbf[s % 2]
                v_ts(xb[:], t, rr_[:, s:s + 1], None, ALU.mult, reads=[b, rr_b], writes=[xb_b])

        def transposes(self, subs, banks=None, ceng=None):
            hT, hT_b = hTs[self.sl]
            for s in subs:
                xb, xb_b = xbf[s % 2]
                bk = self._bank(banks)
                pv = bank_bf(bk)
                for kc in range(8):
                    tr(pv[:, kc * 128:(kc + 1) * 128], xb[:, kc * 128:(kc + 1) * 128], reads=[xb_b], writes=[bank_b[bk]],
                       signal=(kc == 7))
                v_copy(hT[:, :, s * 128:(s + 1) * 128], pv.rearrange("p (k t) -> p k t", k=8),
                       reads=[bank_b[bk]], writes=[hT_b], eng=(ceng or ("act" if s % 2 else "dve")))

        def tables(self):
            if self.rope_ap is None:
                return
            sl, gsel = self.sl, self.gsel
            rt_, rt_b = ropes[sl]
            tb, tb_b = tabss[sl]
            ge = geo[:, gsel * 64:gsel * 64 + 32].unsqueeze(1).broadcast_to([128, 4, 32])
            go = geo[:, gsel * 64 + 32:gsel * 64 + 64].unsqueeze(1).broadcast_to([128, 4, 32])
            cs, sn = rt_[:, :, 0:32], rt_[:, :, 32:64]
            for k, (a_, g_) in enumerate(((cs, ge), (sn, go), (sn, ge), (cs, go))):
                v_tt(tb[:, k, :, :], a_, g_, ALU.mult, reads=[rt_b, geo_b], writes=[tb_b], eng="pool")

        def halo_load(self):
            if self.halo_aps is None:
                return
            xh_t, xh_b, xh_i = next_xt()
            P.dma("sp", xh_t[0:8, :], self.halo_aps[0], f"d_xt{xh_i}", writes=[xh_b])
            P.dma("sp", xh_t[8:16, :], self.halo_aps[1], f"d_xt{xh_i}", writes=[xh_b], join=True)
            self.xh = (xh_t[0:16, :], xh_b)

        def halo_compute(self):
            if self.halo_aps is None:
                return
            if self.xh is None:
                self.halo_load()
            xh, xh_b = self.xh
            (hs, hs_b), (hv, hv_b), (hr, hr_b) = sth
            xhbf, xhbf_b = xbf[0][0][0:16, :], xbf[0][1]
            P.op("act", lambda: nc.scalar.activation(out=xhbf, in_=xh, func=ACT.Square, accum_out=hs[:]),
                 reads=[xh_b], writes=[xhbf_b, hs_b])
            rsqrt_small(hs[:], hv[:], hr[:], mhalf[0:16, 0:1], 1.0 / D, EPS, hs_b, hv_b, hr_b)
            v_ts(xhbf, xh, hr[:], None, ALU.mult, reads=[xh_b, hr_b], writes=[xhbf_b])

        def halo_transposes(self, banks=None):
            if self.halo_aps is None:
                return
            hTh, hTh_b = hThs[self.sl]
            xhbf, xhbf_b = xbf[0][0][0:16, :], xbf[0][1]
            bk = self._bank(banks)
            pv = bank_bf(bk)
            for kc in range(8):
                tr(pv[:, kc * 16:(kc + 1) * 16], xhbf[:, kc * 128:(kc + 1) * 128], reads=[xhbf_b], writes=[bank_b[bk]], signal=(kc == 7))
            v_copy(hTh[:].rearrange("p k t -> p (k t)"), pv[:, 0:128], reads=[bank_b[bk]], writes=[hTh_b])

        def stage_end(self, k):
            if k == 0:
                self.loads()
            elif k == 1:
                self.squares()
                self.scale([0, 1])
                self.halo_load()
            elif k == 2:
                self.scale([2, 3])
                self.tables()

        def stage_mid(self, k):
            if k == 2:
                self.transposes([0, 1], banks=[6, 7], ceng="dve")
            elif k == 3:
                self.transposes([2, 3], banks=[6, 7], ceng="dve")
                self.halo_compute()
                self.halo_transposes(banks=[6])

        def all(self):
            self.start()
            for s in range(self.nsub):
                self.scale([s])
                self.transposes([s])
            self.tables()
            self.halo_compute()
            self.halo_transposes()
            return self.sl

    def prep(x_rows_ap, nsub, **kw):
        return Prep(x_rows_ap, nsub, **kw).all()

    def norm_rope(src4, src_bufs, H, sl, part=0):
        n = 4 * H * 64
        NH = 4 * H
        (ss, ss_b), (vv, vv_b), (rs_, rs_b) = st32
        tb, tb_b = tabss[sl]
        if part != 2:
            sq3 = ovA[:, 0:n].rearrange("p (s c) -> p s c", s=4)
            src3 = src4.rearrange("p s h d -> p s (h d)")
            P.op("act", lambda: nc.scalar.activation(out=sq3, in_=src3, func=ACT.Square), reads=src_bufs, writes=[sqr_b])
            P.op("dve", lambda: nc.vector.tensor_reduce(out=ss[:, 0:NH], in_=ovA[:, 0:n].rearrange("p (a d) -> p a d", d=64), axis=AX.X, op=ALU.add),
                 reads=[sqr_b], writes=[ss_b])
            rsqrt_small(ss[:, 0:NH], vv[:, 0:NH], rs_[:, 0:NH], mhalf[:, 0:NH], 1.0 / 64, EPS, ss_b, vv_b, rs_b)
        if part == 1:
            return
        xn4 = ovB[:, 0:n].rearrange("p (s h d) -> p s h d", s=4, h=H)
        rsb = rs_[:, 0:NH].rearrange("p (s h) -> p s h", s=4).unsqueeze(3).broadcast_to([128, 4, H, 64])
        v_tt(xn4, src4, rsb, ALU.mult, reads=list(src_bufs) + [rs_b], writes=[xqn_b])
        xv = ovB[:, 0:n].rearrange("p (s h i two) -> p s h i two", s=4, h=H, two=2)
        ov = qbf[:, 0:n].rearrange("p (s h i two) -> p s h i two", s=4, h=H, two=2)
        x0, x1 = xv[:, :, :, :, 0], xv[:, :, :, :, 1]
        o0, o1 = ov[:, :, :, :, 0], ov[:, :, :, :, 1]
        hn = n // 2
        a = ovA[:, 0:hn].rearrange("p (s h i) -> p s h i", s=4, h=H)
        bb = ovA[:, hn:2 * hn].rearrange("p (s h i) -> p s h i", s=4, h=H)
        Tb = [tb[:, k, :, :].unsqueeze(2).broadcast_to([128, 4, H, 32]) for k in range(4)]
        rd = [xqn_b, tb_b]
        v_tt(a, x0, Tb[0], ALU.mult, reads=rd + [sqr_b], writes=[sqr_b])
        v_tt(bb, x1, Tb[1], ALU.mult, reads=rd, writes=[sqr_b])
        v_tt(o0, a, bb, ALU.subtract, reads=[sqr_b], writes=[qbf_b])
        v_tt(a, x0, Tb[2], ALU.mult, reads=rd, writes=[sqr_b])
        v_tt(bb, x1, Tb[3], ALU.mult, reads=rd, writes=[sqr_b])
        v_tt(o1, a, bb, ALU.add, reads=[sqr_b], writes=[qbf_b])

    def early_overlay():
        P.inherit([sqr_b], [m[1] for m in macc])
        P.inherit([xqn_b], [mT_b])
        P.inherit([uT_b], [y[1] for y in ytmps])

    def mem_phase(mem_ap):
        sl = prep(mem_ap, 2)
        hT, hT_b = hTs[sl]
        wk, wk_b = ring_load(U_MK)
        wv, wv_b = ring_load(U_MV)
        wkv_ = wk[:].rearrange("p (k c) -> p k c", k=8)
        wvv_ = wv[:].rearrange("p (k c) -> p k c", k=8)
        for c in range(4):
            bk = next_bank()
            for kc in range(8):
                mm(bank(bk)[:, 0:NMEM], wkv_[:, kc, c * 128:(c + 1) * 128], hT[:, kc, 0:NMEM], kc == 0, kc == 7,
                   reads=[wk_b, hT_b], writes=[bank_b[bk]])
            v_copy(memKT[:, c, :], bank(bk)[:, 0:NMEM], reads=[bank_b[bk]], writes=[memKT_b], eng=("act" if c % 2 else "dve"))
        for mc in range(2):
            bk = next_bank()
            for kc in range(8):
                mm(bank(bk), hT[:, kc, mc * 128:(mc + 1) * 128], wvv_[:, kc, :], kc == 0, kc == 7,
                   reads=[wv_b, hT_b], writes=[bank_b[bk]])
            v_copy(memV[:, mc, :], bank(bk), reads=[bank_b[bk]], writes=[memV_b], eng=("act" if mc % 2 else "dve"))

    def kv_phase(xkv_ap, rope_ap, Lkv, after_last=None):
        nt = Lkv // TQ
        early_overlay()
        for n in range(3):
            P.inherit([xtx[n][1]], br_cb[n])
        sl = prep(xkv_ap[0:TQ, :], 4, rope_ap=rope_ap[0:TQ, :], gsel=1, kv=True)
        for ti in range(nt):
            t0 = ti * TQ
            hT, hT_b = hTs[sl]
            bk = next_pair()
            for s in range(4):
                o = ps[:, bk * 512 + s * 256: bk * 512 + (s + 1) * 256]
                bb = bank_b[bk + s // 2]
                for kc in range(8):
                    mm(o, hT[:, kc, s * 128:(s + 1) * 128], Wkv[:, kc, :], kc == 0, kc == 7, reads=[hT_b, Wkv_b], writes=[bb])
            cur_sl = sl
            if ti + 1 < nt:
                sl = prep(xkv_ap[t0 + TQ:t0 + 2 * TQ, :], 4, rope_ap=rope_ap[t0 + TQ:t0 + 2 * TQ, :], gsel=1, kv=True)
            elif after_last is not None:
                sl = after_last()
            kvv = ps[:, bk * 512:(bk + 2) * 512].rearrange("p (s c) -> p s c", s=4)
            bks = [bank_b[bk], bank_b[bk + 1]]
            v_copy(VA[:, ti * 4:(ti + 1) * 4, 0:64], kvv[:, :, 128:192], reads=bks, writes=[VA_b])
            v_copy(VA[:, ti * 4:(ti + 1) * 4, 128:192], kvv[:, :, 192:256], reads=bks, writes=[VA_b])
            norm_rope(kvv[:, :, 0:128].rearrange("p s (h d) -> p s h d", h=2), bks, 2, cur_sl)
            bk2 = next_bank()
            pv = bank_bf(bk2)
            for s in range(4):
                tr(pv[:, s * 128:(s + 1) * 128], qbf[:, s * 128:(s + 1) * 128], reads=[qbf_b], writes=[bank_b[bk2]], signal=(s == 3))
            v_copy(KT[:, t0:t0 + TQ], pv[:, 0:512], reads=[bank_b[bk2]], writes=[KT_b], eng="act")
        return sl

    def prep_q(seg, ti):
        t0 = ti * TQ
        xq_ap = seg["xq"]
        return Prep(xq_ap[t0 + 8:t0 + 8 + TQ, :], 4, rope_ap=seg["rope_q"][t0:t0 + TQ, :], gsel=0, edge_ap=seg["edge"][ti],
                    halo_aps=(xq_ap[t0:t0 + 8, :], xq_ap[t0 + 8 + TQ:t0 + 16 + TQ, :]))

    def q_tile(seg, ti, sl, next_prep):
        Lkv = seg["Lkv"]
        t0 = ti * TQ
        hT, hT_b = hTs[sl]
        hTh, hTh_b = hThs[sl]
        edge_t, edge_t_b = edges[sl]
        early_overlay()
        if ti == 0:
            for n in range(3):
                P.inherit(br_cb[n], [xtx[n][1]])

        def fm_unit(unit, evac):
            w, w_b = ring_load(unit)
            wv_ = w[:].rearrange("p (k c) -> p k c", k=8)
            for c in range(4):
                bk = next_bank()
                for kc in range(8):
                    mm(bank(bk), wv_[:, kc, c * 128:(c + 1) * 128], hT[:, kc, :], kc == 0, kc == 7, reads=[w_b, hT_b], writes=[bank_b[bk]])
                evac(c, bk)
            return w, w_b, wv_

        gi = [0]

        def silu_gate_evac(n):
            def ev(c, bk):
                g, g_b = gtmp[gi[0] % 2]
                gi[0] += 1
                P.op("act", lambda: nc.scalar.activation(out=g[:], in_=bank(bk), func=ACT.Tanh, scale=0.5), reads=[bank_b[bk]], writes=[g_b])
                v_stt(br[n][0][:, c, :], g[:], 1.0, bank(bk), ALU.add, ALU.mult, reads=[g_b, bank_b[bk]], writes=[br_cb[n][c]])
            return ev

        w, w_b = ring_load(U_Q)
        wv_ = w[:].rearrange("p (k c) -> p k c", k=8)
        qb = next_quad()
        for s in range(4):
            for kc in range(8):
                mm(bank(qb + s), hT[:, kc, s * 128:(s + 1) * 128], wv_[:, kc, :], kc == 0, kc == 7, reads=[w_b, hT_b], writes=[bank_b[qb + s]])
        qbks = [bank_b[qb + s] for s in range(4)]
        q4 = bank(qb, 4).rearrange("p (s h d) -> p s h d", s=4, h=8)

        def ev_xq(c, bk):
            v_copy(xqT[:, c, :], bank(bk), reads=[bank_b[bk]], writes=[xqT_b], eng=("act" if c % 2 else "dve"))
        fm_unit(U_XQ, ev_xq)
        norm_rope(q4, qbks, 8, sl, part=1)

        nmax, nmax_b = xst[0]
        nb_, nb_b = xst[1]
        sume, sume_b = xst[2]
        rs_, rs_b = xst[3]
        xscale = 128 ** -0.5

        def cross_stage1(wave):
            bk = next_quad(avoid=(qb if wave == 0 else None))
            for hh in range(2):
                h = wave * 2 + hh
                for s in range(4):
                    o = ps[:, (bk + hh * 2) * 512 + s * 256: (bk + hh * 2) * 512 + (s + 1) * 256]
                    mm(o, xqT[:, h, s * 128:(s + 1) * 128], memKT[:, h, :], True, True, reads=[xqT_b, memKT_b],
                       writes=[bank_b[bk + hh * 2 + s // 2]])
            sv = bank(bk, 4).rearrange("p (a m) -> p a m", a=8)
            bks = [bank_b[bk + i] for i in range(4)]
            P.op("dve", lambda: nc.vector.tensor_reduce(out=nmax[:], in_=sv, axis=AX.X, op=ALU.max, negate=True), reads=bks, writes=[nmax_b])
            v_ts(nb_[:], nmax[:], xscale, None, ALU.mult, reads=[nmax_b], writes=[nb_b])
            for a in range(8):
                P.op("act", lambda a=a: nc.scalar.activation(out=pexp[:, a, :], in_=sv[:, a, :], func=ACT.Exp, bias=nb_[:, a:a + 1],
                                                              scale=xscale, accum_out=sume[:, a:a + 1]),
                     reads=bks + [nb_b], writes=[pexp_b, sume_b])

        def cross_stage1b():
            P.op("dve", lambda: nc.vector.reciprocal(out=rs_[:], in_=sume[:]), reads=[sume_b], writes=[rs_b])
            v_tt(pexp[:], pexp[:], rs_[:].unsqueeze(2).broadcast_to([128, 8, NMEM]), ALU.mult, reads=[pexp_b, rs_b], writes=[pexp_b])

        def cross_stage2(wave):
            bk2 = next_pair()
            for hh in range(2):
                pv = bank_bf(bk2 + hh)
                for s in range(4):
                    for mc in range(2):
                        i = s * 2 + mc
                        tr(pv[:, i * 128:(i + 1) * 128], pexp[:, hh * 4 + s, mc * 128:(mc + 1) * 128], reads=[pexp_b],
                           writes=[bank_b[bk2 + hh]], signal=(i == 7))
                v_copy(pTt[:, hh * 8:(hh + 1) * 8, :].rearrange("p a t -> p (a t)"), pv, reads=[bank_b[bk2 + hh]], writes=[pTt_b], eng="act")
            bk3 = next_pair()
            for hh in range(2):
                h = wave * 2 + hh
                for s in range(4):
                    for mc in range(2):
                        mm(bank(bk3 + hh)[:, s * 128:(s + 1) * 128], memV[:, mc, h * 128:(h + 1) * 128], pTt[:, hh * 8 + s * 2 + mc, :],
                           mc == 0, mc == 1, reads=[memV_b, pTt_b], writes=[bank_b[bk3 + hh]], signal=(mc == 1 and s == 3))
                v_tt(br[2][0][:, h, :], bank(bk3 + hh), br[2][0][:, h, :], ALU.mult, reads=[bank_b[bk3 + hh], br_cb[2][h]], writes=[br_cb[2][h]])

        cross_stage1(0)
        norm_rope(q4, qbks, 8, sl, part=2)
        cross_stage1b()
        fm_unit(U_XGATE, silu_gate_evac(2))
        cross_stage2(0)
        cross_stage1(1)

        def ev_pool_in(c, bk):
            v_copy(uT[:, c, 8:8 + TQ], bank(bk), reads=[bank_b[bk]], writes=[uT_b], eng=("act" if c % 2 else "dve"))
        w, w_b, wv_ = fm_unit(U_POOL_IN, ev_pool_in)
        bk = next_bank()
        for c in range(4):
            for kc in range(8):
                mm(bank(bk)[:, c * 16:(c + 1) * 16], wv_[:, kc, c * 128:(c + 1) * 128], hTh[:, kc, :], kc == 0, kc == 7,
                   reads=[w_b, hTh_b], writes=[bank_b[bk]], signal=(kc == 7 and c == 3))
        hv = bank(bk)[:, 0:64].rearrange("p (c t) -> p c t", c=4)
        v_copy(uT[:, :, 0:8], hv[:, :, 0:8], reads=[bank_b[bk]], writes=[uT_b])
        v_copy(uT[:, :, 8 + TQ:16 + TQ], hv[:, :, 8:16], reads=[bank_b[bk]], writes=[uT_b])

        cross_stage1b()
        for half in range(2):
            bk2 = next_bank()
            pv = bank_bf(bk2)
            for s2 in range(2):
                s = half * 2 + s2
                for b_ in range(4):
                    tr(pv[:, (s2 * 4 + b_) * 128:(s2 * 4 + b_ + 1) * 128], qbf[:, s * 512 + b_ * 128: s * 512 + (b_ + 1) * 128],
                       reads=[qbf_b], writes=[bank_b[bk2]], signal=(s2 == 1 and b_ == 3))
            v_copy(QT[:, :, half * 256:(half + 1) * 256].rearrange("p b (s t) -> p b s t", s=2),
                   pv.rearrange("p (s b t) -> p b s t", s=2, b=4), reads=[bank_b[bk2]], writes=[QT_b], eng="act")

        cross_stage2(1)
        w_pg = ring_load(U_POOL_GATE)

        for g, wdw in enumerate(POOL_W):
            U = uT[:, g, :]
            n2 = TQ + 15
            v_tt(pa[:, 0:n2], U[:, 0:n2], U[:, 1:n2 + 1], ALU.add, reads=[uT_b], writes=[pa_b], eng="pool")
            cur, cur_b, oth, oth_b, n = pa, pa_b, pb, pb_b, n2
            step = 2
            while step < wdw:
                n = n - step
                v_tt(oth[:, 0:n], cur[:, 0:n], cur[:, step:step + n], ALU.add, reads=[cur_b], writes=[oth_b], eng="pool")
                cur, cur_b, oth, oth_b = oth, oth_b, cur, cur_b
                step *= 2
            o = 8 - wdw // 2
            S = cur[:, o:o + TQ]
            v_tt(S[:, 0:8], S[:, 0:8], edge_t[:, g * 16:g * 16 + 8], ALU.mult, reads=[cur_b, edge_t_b], writes=[cur_b])
            v_tt(S[:, TQ - 8:TQ], S[:, TQ - 8:TQ], edge_t[:, g * 16 + 8:g * 16 + 16], ALU.mult, reads=[cur_b, edge_t_b], writes=[cur_b])
            v_stt(mixT[:, g, :], S, 1.0 / wdw, U[:, 8:8 + TQ], ALU.mult, ALU.subtract, reads=[cur_b, uT_b], writes=[mixT_b])

        w_ag = ring_load(U_ATTN_GATE)

        def gate_unit_steps(wb_, n):
            w, w_b = wb_
            wv_ = w[:].rearrange("p (k c) -> p k c", k=8)
            ev = silu_gate_evac(n)
            for c in range(4):
                bk = 6 + (c % 2)
                for kc in range(8):
                    def gstep(c=c, kc=kc, bk=bk):
                        mm(bank(bk), wv_[:, kc, c * 128:(c + 1) * 128], hT[:, kc, :], kc == 0, kc == 7, reads=[w_b, hT_b], writes=[bank_b[bk]])
                        if kc == 7:
                            ev(c, bk)
                    yield (kc == 0, gstep)

        def pool_mm_steps():
            for g in range(4):
                def pstep_(g=g):
                    bk = 6 + (g % 2)
                    mm(bank(bk), Wpool[:, g, :], mixT[:, g, :], True, True, reads=[Wpool_b, mixT_b], writes=[bank_b[bk]])
                    v_tt(br[0][0][:, g, :], bank(bk), br[0][0][:, g, :], ALU.mult, reads=[bank_b[bk], br_cb[0][g]], writes=[br_cb[0][g]])
                yield (True, pstep_)
        pre_steps = list(gate_unit_steps(w_ag, 1)) + list(gate_unit_steps(w_pg, 0)) + list(pool_mm_steps())
        n_gate1 = 32

        P.inherit([m[1] for m in macc], [sqr_b])
        P.inherit([mT_b], [xqn_b])

        in_att = [True]
        wo_pref = {}

        def merge_steps():
            units = [(0, 0), (2, 0), (0, 1), (2, 1), (1, 0), (1, 1)]
            loaded = {}

            def load(k, which):
                if k < len(units) and (k, which) not in loaded:
                    n, half = units[k]
                    if which == "l":
                        loaded[(k, which)] = ring_load(U_MERGE0 + n * 2 + half)
                    else:
                        loaded[(k, which)] = ring_load(U_BR0 + n * 2 + half, 2048)
            st = {}
            for k, (n, half) in enumerate(units):
                for c in range(4):
                    cc = half * 4 + c
                    for kc in range(8):
                        def lstep(k=k, n=n, c=c, cc=cc, kc=kc):
                            if c == 0 and kc == 0:
                                load(k, "l")
                                load(k, "b")
                                load(k + 1, "l")
                            wl, wl_b = loaded[(k, "l")]
                            wlv = wl[:].rearrange("p (k c) -> p k c", k=8)
                            if c == 0 and kc == 0 and k == len(units) - 1:
                                wo_pref[0] = ring_load(U_OUT0)
                            if kc == 0:
                                st["bl"] = 6 if in_att[0] else next_bank()
                            bl = st["bl"]
                            mm(bank(bl), wlv[:, kc, c * 128:(c + 1) * 128], hT[:, kc, :], kc == 0, kc == 7, reads=[wl_b, hT_b], writes=[bank_b[bl]])
                            if kc == 7:
                                g, g_b = gtmp[gi[0] % 2]
                                gi[0] += 1
                                st["g"] = (g, g_b)
                                P.op("act", lambda: nc.scalar.activation(out=g[:], in_=bank(bl), func=ACT.Tanh,
                                                                         bias=bmh[:, n * 8 + cc:n * 8 + cc + 1], scale=0.5),
                                     reads=[bank_b[bl], bmh_b], writes=[g_b])
                                if c == 3:
                                    load(k + 1, "b")
                                    if k == len(units) - 1:
                                        wo_pref[1] = ring_load(U_OUT0 + 1)
                        yield (kc == 0, lstep)
                    for kc in range(4):
                        def pstep(k=k, n=n, c=c, cc=cc, kc=kc):
                            wb, wb_b = loaded[(k, "b")]
                            wbv = wb[:, 0:2048].rearrange("p (k c) -> p k c", k=4)
                            if kc == 0:
                                st["bp"] = 7 if in_att[0] else next_bank()
                            bp = st["bp"]
                            mm(bank(bp), wbv[:, kc, c * 128:(c + 1) * 128], br[n][0][:, kc, :], kc == 0, kc == 3,
                               reads=[wb_b, br_cb[n][kc]], writes=[bank_b[bp]])
                            if kc == 3:
                                g, g_b = st["g"]
                                ma, ma_b = macc[c]
                                if n == 0:
                                    v_stt(ma, g[:], 1.0, bank(bp), ALU.add, ALU.mult, reads=[g_b, bank_b[bp]], writes=[ma_b])
                                elif n == 2:
                                    v_stt(ptmp2[:], g[:], 1.0, bank(bp), ALU.add, ALU.mult, reads=[g_b, bank_b[bp]], writes=[ptmp2_b])
                                    v_tt(mT[:, cc, :], ma, ptmp2[:], ALU.add, reads=[ma_b, ptmp2_b], writes=[mT_b], eng="pool")
                                else:
                                    v_stt(ptmp2[:], g[:], 1.0, bank(bp), ALU.add, ALU.mult, reads=[g_b, bank_b[bp]], writes=[ptmp2_b])
                                    v_tt(mT[:, cc, :], mT[:, cc, :], ptmp2[:], ALU.add, reads=[mT_b, ptmp2_b], writes=[mT_b], eng="pool")
                        yield (False, pstep)

        nj = Lkv // 128
        nsl = None
        ptmp2, ptmp2_b = gp, gp_b
        rcA, rcA_b = ptmp, ptmp_b
        rcB, rcB_b = pb[:, 0:TQ], pb_b
        g2, g2_b = pa[:, 0:TQ], pa_b
        all_steps = pre_steps + list(merge_steps())
        n_att_steps = len(pre_steps) + 2 * 4 * 12
        steps = all_steps[:n_att_steps]
        n_it = 4 * nj
        rate = len(steps) / float(n_it)
        owed = 0.0
        done_steps = 0
        pending_mid = None
        accA, accB = 4, 5
        for b_ in range(4):
            def qk(j):
                sb_ = 0 if j % 2 == 0 else 2
                mm(bank(sb_), KT[0:64, j * 128:(j + 1) * 128], QT[0:64, b_, :], True, True, reads=[KT_b, QT_b], writes=[bank_b[sb_]])
                mm(bank(sb_ + 1), KT[64:128, j * 128:(j + 1) * 128], QT[64:128, b_, :], True, True, reads=[KT_b, QT_b], writes=[bank_b[sb_ + 1]])

            def ex(j):
                sb_ = 0 if j % 2 == 0 else 2
                pt, pt_b = PT[j % NPT]
                P.op("act", lambda: nc.scalar.activation(out=pt[:], in_=bank(sb_, 2), func=ACT.Exp, bias=negM[:], scale=0.125),
                     reads=[bank_b[sb_], bank_b[sb_ + 1], negM_b], writes=[pt_b])

            def pvm(j):
                pt, pt_b = PT[j % NPT]
                mm(bank(accA), VA[:, j, 0:128], pt[:, 0:TQ], j == 0, j == nj - 1, reads=[VA_b, pt_b], writes=[bank_b[accA]])
                mm(bank(accB), VA[:, j, 64:192], pt[:, TQ:2 * TQ], j == 0, j == nj - 1, reads=[VA_b, pt_b], writes=[bank_b[accB]])

            qk(0)
            if nj > 1:
                qk(1)
            for j in range(nj):
                if next_prep is not None and j == nj // 2:
                    pending_mid = b_
                ex(j)
                if j + 2 < nj:
                    qk(j + 2)
                pvm(j)
                owed += rate
                while owed >= 1.0 and done_steps < len(steps):
                    steps[done_steps][1]()
                    done_steps += 1
                    owed -= 1.0
                if pending_mid is not None and (done_steps >= len(steps) or steps[done_steps][0]):
                    next_prep.stage_mid(pending_mid)
                    pending_mid = None
            while done_steps < n_gate1:
                steps[done_steps][1]()
                done_steps += 1
            if pending_mid is not None:
                while done_steps < len(steps) and not steps[done_steps][0]:
                    steps[done_steps][1]()
                    done_steps += 1
                next_prep.stage_mid(pending_mid)
                pending_mid = None
            v_copy(rcA[:], bank(accA), reads=[bank_b[accA]], writes=[rcA_b])
            v_copy(rcB, bank(accB), reads=[bank_b[accB]], writes=[rcB_b])
            P.op("dve", lambda: nc.vector.reciprocal(out=rc[0:64, :], in_=rcA[64:128, :]), reads=[rcA_b], writes=[rc_b])
            P.op("dve", lambda: nc.vector.reciprocal(out=rc[64:128, :], in_=rcB[0:64, :]), reads=[rcB_b], writes=[rc_b])
            v_tt(g2, br[1][0][:, b_, :], rc[:], ALU.mult, reads=[rc_b, br_cb[1][b_]], writes=[g2_b], eng="pool")
            v_tt(br[1][0][0:64, b_, :], rcA[0:64, :], g2[0:64, :], ALU.mult, reads=[rcA_b, g2_b], writes=[br_cb[1][b_]])
            v_tt(br[1][0][64:128, b_, :], rcB[64:128, :], g2[64:128, :], ALU.mult, reads=[rcB_b, g2_b], writes=[br_cb[1][b_]])
            if next_prep is not None:
                next_prep.stage_end(b_)
                nsl = next_prep.sl
        while done_steps < len(steps):
            steps[done_steps][1]()
            done_steps += 1

        xr = []
        for s in range(4):
            t, b, i = next_xt()
            P.dma("sp", t, seg["xq"][t0 + 8 + s * 128:t0 + 8 + (s + 1) * 128, :], f"d_xt{i}", writes=[b])
            xr.append((t, b, i))
        in_att[0] = False
        for _, step in all_steps[n_att_steps:]:
            step()

        P.inherit([y[1] for y in ytmps], [uT_b])
        wo = [wo_pref[0], wo_pref[1]]
        (ss, ss_b), (vv, vv_b), (rr_, rr_b) = stxs[sl]
        rr[0] = 0
        for half in range(2):
            w, w_b = wo[half]
            wv_ = w[:].rearrange("p (k c) -> p k c", k=8)
            for s in range(4):
                for kc in range(8):
                    mm(bank(2 * s + half), mT[:, kc, s * 128:(s + 1) * 128], wv_[:, kc, :], kc == 0, kc == 7, reads=[w_b, mT_b],
                       writes=[bank_b[2 * s + half]])
        for s in range(4):
            bk = 2 * s
            bks = [bank_b[bk], bank_b[bk + 1]]
            z = bank(bk, 2)
            yt, yt_b = ytmps[s % 2]
            P.op("act", lambda z=z, s=s, yt=yt: nc.scalar.activation(out=yt, in_=z, func=ACT.Square, accum_out=ss[:, s:s + 1]),
                 reads=bks, writes=[yt_b, ss_b])
            rsqrt_small(ss[:, s:s + 1], vv[:, s:s + 1], rr_[:, s:s + 1], mhalf[:, 0:1], 1.0 / D, 16.0 * EPS, ss_b, vv_b, rr_b)
            v_stt(yt, z, rr_[:, s:s + 1], lnpost[:], ALU.mult, ALU.mult, reads=bks + [rr_b, lnpost_b], writes=[yt_b])
            t, b, i = xr[s]
            v_tt(t, t, yt, ALU.add, reads=[b, yt_b], writes=[b], eng="pool")
        for s in range(4):
            t, b, i = xr[s]
            P.dma("act", seg["y"][t0 + s * 128:t0 + (s + 1) * 128, :], t, f"d_xt{i}", reads=[b])
        return nsl

    segs = []
    for i in range(NS):
        segs.append(dict(xq=xs_d[i], xkv=xs_d[i][8:8 + LS, :], mem=mem_d[i], rope_q=rope_s_d, rope_k=rope_s_d,
                         edge=edge_s_d, y=ys_d[i], Lq=LS, Lkv=LS))
    if LQP > 0:
        segs.append(dict(xq=xqp_d, xkv=xp_d, mem=mem_d[NS], rope_q=rope_qp_d, rope_k=rope_p_d,
                         edge=edge_qp_d, y=yp_d, Lq=LQP, Lkv=LKP))
    for seg in segs:
        mem_phase(seg["mem"])
        sl = kv_phase(seg["xkv"], seg["rope_k"], seg["Lkv"], after_last=lambda seg=seg: prep_q(seg, 0).all())
        nq = seg["Lq"] // TQ
        for ti in range(nq):
            nxt = prep_q(seg, ti + 1) if ti + 1 < nq else None
            sl = q_tile(seg, ti, sl, nxt)

    P.wait_all("sp", [b for (_, b) in xt] + [b for (_, b) in xtx])
    P.wait_all("act", [b for (_, b) in xt] + [b for (_, b) in xtx])
    P.finish()
    return nc, P


def rope_table(L):
    rows = L // 64
    row = np.repeat(np.arange(rows, dtype=np.float32), 64)
    col = np.tile(np.arange(64, dtype=np.float32), rows)
    inv = (np.float32(10000.0) ** (-np.arange(0, 32, 2, dtype=np.float32) / np.float32(32))).astype(np.float32)
    ang = np.concatenate([row[:, None] * inv, col[:, None] * inv], axis=-1).astype(np.float32)
    return np.concatenate([np.cos(ang), np.sin(ang)], axis=-1).astype(np.float32)


def edge_table(L, q0, Lq):
    nt = Lq // TQ
    out = np.ones((nt, 4, 16), np.float32)
    for ti in range(nt):
        for g, w in enumerate(POOL_W):
            for k in range(16):
                t = q0 + ti * TQ + (k if k < 8 else TQ - 16 + k)
                lo = min(max(t - w // 2, 0), L)
                hi = min(max(t + (w - 1 - w // 2) + 1, 0), L)
                out[ti, g, k] = np.float32(w) / np.float32(hi - lo)
    return np.ascontiguousarray(np.broadcast_to(out.reshape(nt, 1, 64), (nt, 128, 64)))


def make_core_inputs(inp, core, cfg):
    NS, LS, LKP, LQP = cfg["NS"], cfg["LS"], cfg["LKP"], cfg["LQP"]
    f = np.float32
    xs_full = inp["x_sample"]
    xp_full = inp["x_prompt"]
    xs = np.zeros((NS, LS + 16, D), f)
    xs[:, 8:8 + LS] = xs_full[core * NS:(core + 1) * NS]
    npq = LKP // LQP if LQP else 1
    pi, qi = divmod(core, npq)
    xp = np.ascontiguousarray(xp_full[pi])
    q0 = qi * LQP
    xpad = np.zeros((LKP + 16, D), f)
    xpad[8:8 + LKP] = xp
    xqp = np.ascontiguousarray(xpad[q0:q0 + LQP + 16])
    mem = np.concatenate([inp["mem_sample"][core * NS:(core + 1) * NS], inp["mem_prompt"][pi:pi + 1]], 0)
    rope_s = rope_table(LS)
    rope_p = rope_table(LKP)
    d = {
        "xs": xs, "xp": xp, "xqp": xqp, "mem": np.ascontiguousarray(mem, f),
        "rope_s": rope_s, "rope_p": rope_p, "rope_qp": np.ascontiguousarray(rope_p[q0:q0 + LQP]),
        "edge_s": edge_table(LS, 0, LS), "edge_qp": edge_table(LKP, q0, LQP),
        "w_in": inp["w_in"], "w_mem_kv": inp["w_mem_kv"], "w_branch": inp["w_branch"], "w_out": inp["w_out"],
        "w_pool": inp["w_pool"],
        "lnpre_l": np.ascontiguousarray(inp["ln_pre"].reshape(8, 128).T),
        "lnmem_l": np.ascontiguousarray(inp["ln_mem"].reshape(8, 128).T),
        "bmerge_l": np.ascontiguousarray(inp["b_merge"].reshape(3, 8, 128).transpose(2, 0, 1).reshape(128, 24)),
        "lnpost_b": np.ascontiguousarray(np.broadcast_to(inp["ln_post"][None, :], (128, D))),
        "geo": np.ascontiguousarray(np.broadcast_to(np.concatenate(
            [inp["q_norm"][0::2], inp["q_norm"][1::2], inp["k_norm"][0::2], inp["k_norm"][1::2]])[None, :], (128, 128))),
        "pscale_b": np.ascontiguousarray(np.broadcast_to(inp["pool_scale"][None, :], (128, 512))),
        "ident": np.eye(128, dtype=f),
    }
    return {k: np.ascontiguousarray(v, dtype=f) for k, v in d.items()}


CFG_FULL = dict(NS=2, LS=4096, LKP=8192, LQP=2048)
_CACHE = {}


def kernel(**inputs):
    inp = {k: np.asarray(v) for k, v in inputs.items()}
    cfg = CFG_FULL
    if "nc" not in _CACHE:
        _CACHE["nc"] = build(cfg)[0]
    nc = _CACHE["nc"]
    in_maps = [make_core_inputs(inp, c, cfg) for c in range(N_CORES)]
    res = run_bass_kernel_spmd(nc, in_maps, core_ids=list(range(N_CORES)))
    y_s = np.concatenate([np.asarray(r["ys"]) for r in res.results], 0).astype(np.float32)
    B, L = inp["x_prompt"].shape[:2]
    y_p = np.empty((B, L, D), np.float32)
    npq = cfg["LKP"] // cfg["LQP"]
    for c in range(N_CORES):
        pi, qi = divmod(c, npq)
        y_p[pi, qi * cfg["LQP"]:(qi + 1) * cfg["LQP"]] = np.asarray(res.results[c]["yp"])
    return (y_p, y_s)
```

```python
from contextlib import ExitStack
import numpy as np
import ml_dtypes
import concourse.bass as bass
import concourse.mybir as mybir
from concourse.bass_utils import run_bass_kernel_spmd

F32 = mybir.dt.float32
BF16 = mybir.dt.bfloat16
ACT = mybir.ActivationFunctionType
ALU = mybir.AluOpType
AX = mybir.AxisListType

D = 1024
EPS = 1e-6
NMEM = 256
TQ = 512
POOL_W = (2, 4, 8, 16)
N_CORES = 8


class Buf:
    __slots__ = ("name", "w", "r")

    def __init__(self, name):
        self.name = name
        self.w = None
        self.r = []


class Prog:
    ENG = ("pe", "act", "dve", "pool", "sp")

    def __init__(self, nc):
        self.nc = nc
        self.h = {"pe": nc.tensor, "act": nc.scalar, "dve": nc.vector, "pool": nc.gpsimd, "sp": nc.sync}
        self.ops = {k: [] for k in self.ENG}
        self.ctx = ExitStack()
        self.sems = {}
        self.cnt = {}
        self.waited = {k: {} for k in self.ENG}
        self.pending = {k: False for k in self.ENG}
        for k in self.ENG:
            self.new_sem("e_" + k)
        self.nbuf = 0
        self.n_instr = 0

    def new_sem(self, name):
        self.sems[name] = self.ctx.enter_context(self.nc.semaphore(name))
        self.cnt[name] = 0
        return name

    def buf(self, name=None):
        self.nbuf += 1
        return Buf(name or f"b{self.nbuf}")

    def sb(self, name, shape, dtype):
        return self.ctx.enter_context(self.nc.sbuf_tensor("sb_" + name, list(shape), dtype))

    def _need(self, eng, reads, writes, join_sem=None):
        need = {}

        def add(pt, raw):
            if pt is None:
                return
            sem, val, peng = pt
            if peng is not None and peng == eng and eng == "pe":
                return
            if val > need.get(sem, 0):
                need[sem] = val

        for b in reads:
            add(b.w, True)
        for b in writes:
            if not (join_sem is not None and b.w is not None and b.w[0] == join_sem and b.w[2] is None):
                add(b.w, False)
            for p in b.r:
                add(p, False)
        return need

    def _emit_waits(self, eng, need):
        h = self.h[eng]
        for sem, val in need.items():
            if val > self.waited[eng].get(sem, 0):
                self.waited[eng][sem] = val
                h.wait_ge(self.sems[sem], val)

    def op(self, eng, fn, reads=(), writes=(), signal=True):
        self.n_instr += 1
        need = self._need(eng, reads, writes)
        self._emit_waits(eng, need)
        sem = "e_" + eng
        if signal:
            self.cnt[sem] += 1
            val = self.cnt[sem]
            fn().then_inc(self.sems[sem], 1)
            self.pending[eng] = False
        else:
            val = self.cnt[sem] + 1
            fn()
            self.pending[eng] = True
        pt = (sem, val, eng)
        for b in reads:
            b.r.append(pt)
            if len(b.r) > 64:
                b.r = _compact(b.r)
        for b in writes:
            b.w = pt
            b.r = []

    def dma(self, queue, out, in_, sem, reads=(), writes=(), join=False):
        self.n_instr += 1
        need = self._need(None, reads, writes, join_sem=(sem if join else None))
        self._emit_waits(queue, need)
        self.cnt[sem] += 16
        val = self.cnt[sem]
        self.h[queue].dma_start(out=out, in_=in_).then_inc(self.sems[sem], 16)
        pt = (sem, val, None)
        for b in reads:
            b.r.append(pt)
            if len(b.r) > 64:
                b.r = _compact(b.r)
        for b in writes:
            b.w = pt
            b.r = []

    def inherit(self, dst, src):
        pts = []
        for s in src:
            if s.w is not None:
                pts.append(s.w)
            pts.extend(s.r)
        pts = _compact(pts)
        for d in dst:
            d.r = _compact(list(d.r) + pts)

    def wait_all(self, eng, bufs):
        need = {}
        for b in bufs:
            for p in ([b.w] if b.w else []) + list(b.r):
                if p[1] > need.get(p[0], 0):
                    need[p[0]] = p[1]
        self._emit_waits(eng, need)

    def finish(self):
        for e in self.ENG:
            assert not self.pending[e], f"engine {e} has unsignalled trailing instruction"
        self.ctx.close()


def _compact(pts):
    best = {}
    for p in pts:
        k = (p[0], p[2])
        if k not in best or p[1] > best[k][1]:
            best[k] = p
    return list(best.values())


U_POOL_IN, U_POOL_GATE, U_Q, U_ATTN_GATE, U_XQ, U_XGATE = 0, 1, 2, 3, 4, 5
U_MERGE0 = 6
U_BR0 = 12
U_OUT0 = 18
U_MK, U_MV = 20, 21
N_UNITS = 22


import os as _os
_PROBE = {k: True for k in _os.environ.get("KPROBE", "").split(",") if k}


class _Stop(Exception):
    pass


def build(cfg):
    NS, LS, LKP, LQP = cfg["NS"], cfg["LS"], cfg["LKP"], cfg["LQP"]

    def ckpt(k):
        if cfg.get("stop") == k:
            raise _Stop()

    LKMAX = max(LS, LKP)
    nc = bass.Bass("TRN2", target_bir_lowering=False)
    P = Prog(nc)

    def din(name, shape, dt=F32):
        return nc.dram_tensor(name, list(shape), dt, kind="ExternalInput").ap()

    xs_d = din("xs", [NS, LS + 16, D])
    xp_d = din("xp", [LKP, D])
    xqp_d = din("xqp", [LQP + 16, D])
    mem_d = din("mem", [NS + 1, NMEM, D])
    rope_s_d = din("rope_s", [LS, 64])
    rope_p_d = din("rope_p", [LKP, 64])
    rope_qp_d = din("rope_qp", [LQP, 64])
    edge_s_d = din("edge_s", [LS // TQ, 128, 64])
    edge_qp_d = din("edge_qp", [LQP // TQ, 128, 64])
    w_in_d = din("w_in", [D, 6400])
    w_mkv_d = din("w_mem_kv", [D, 1024])
    w_br_d = din("w_branch", [3, 512, D])
    w_out_d = din("w_out", [D, D])
    w_pool_d = din("w_pool", [4, 128, 128])
    lnpre_d = din("lnpre_l", [128, 8])
    lnmem_d = din("lnmem_l", [128, 8])
    bmerge_d = din("bmerge_l", [128, 24])
    lnpost_d = din("lnpost_b", [128, D])
    geo_d = din("geo", [128, 128])
    pscale_d = din("pscale_b", [128, 512])
    ident_d = din("ident", [128, 128])
    ys_d = nc.dram_tensor("ys", [NS, LS, D], F32, kind="ExternalOutput").ap()
    yp_d = nc.dram_tensor("yp", [LQP, D], F32, kind="ExternalOutput").ap()
    scr_d = nc.dram_tensor("wscr", [N_UNITS, 128, 4096], BF16, kind="Internal").ap()
    scr_b = [P.buf(f"scr{u}") for u in range(N_UNITS)]

    ps = P.ctx.enter_context(nc.psum_tensor("ps", [128, 4096], F32))
    bank_b = [P.buf(f"bank{i}") for i in range(8)]

    def bank(i, n=1):
        return ps[:, i * 512:(i + n) * 512]

    def bank_bf(i):
        return ps[:, i * 512:(i + 1) * 512].bitcast(BF16)

    rr = [0]

    def next_bank():
        b = rr[0] % 8
        rr[0] += 1
        return b

    def next_pair():
        if rr[0] % 2:
            rr[0] += 1
        b = rr[0] % 8
        rr[0] += 2
        return b

    def next_quad(avoid=None):
        while rr[0] % 4:
            rr[0] += 1
        if avoid is not None and rr[0] % 8 == avoid:
            rr[0] += 4
        b = rr[0] % 8
        rr[0] += 4
        return b

    def T(name, shape, dt):
        return P.sb(name, shape, dt), P.buf(name)

    ident_f, ident_f_b = T("ident_f", [128, 128], F32)
    ident, ident_b = T("ident", [128, 128], BF16)
    lnpre, lnpre_b = T("lnpre", [128, 8], F32)
    lnmem, lnmem_b = T("lnmem", [128, 8], F32)
    bmh, bmh_b = T("bmh", [128, 24], F32)
    lnpost, lnpost_b = T("lnpost", [128, D], F32)
    geo, geo_b = T("geo", [128, 128], F32)
    pscale, pscale_b = T("pscale", [128, 512], F32)
    negM, negM_b = T("negM", [128, 1], F32)
    mxa, mxa_b = T("mxa", [128, 2], F32)
    mhalf, mhalf_b = T("mhalf", [128, 32], F32)
    Wkv, Wkv_b = T("Wkv", [128, 8, 256], BF16)
    Wpool, Wpool_b = T("Wpool", [128, 4, 128], BF16)
    KT, KT_b = T("KT", [128, LKMAX], BF16)
    VA, VA_b = T("VA", [128, LKMAX // 128, 192], BF16)
    memKT, memKT_b = T("memKT", [128, 4, NMEM], BF16)
    memV, memV_b = T("memV", [128, 2, 512], BF16)
    NRING = 3
    ring = [T(f"wr{i}", [128, 4096], BF16) for i in range(NRING)]
    NXT = 4
    xt = [T(f"xt{i}", [128, D], F32) for i in range(NXT)]
    xbf = [T(f"xbf{i}", [128, D], BF16) for i in range(2)]
    hTs = [T(f"hT{i}", [128, 8, TQ], BF16) for i in range(2)]
    hThs = [T(f"hTh{i}", [128, 8, 16], BF16) for i in range(2)]
    ropes = [T(f"rope_t{i}", [128, 4, 64], F32) for i in range(2)]
    tabss = [T(f"tabs{i}", [128, 4, 4, 32], F32) for i in range(2)]
    edges = [T(f"edge_t{i}", [128, 64], F32) for i in range(2)]
    stxs = [[T(f"stx{j}_{i}", [128, 4], F32) for i in range(3)] for j in range(2)]
    sth = [T(f"sth_{i}", [16, 1], F32) for i in range(3)]
    uT, uT_b = T("uT", [128, 4, TQ + 16], F32)
    pa, pa_b = T("pa", [128, TQ + 16], F32)
    pb, pb_b = T("pb", [128, TQ + 16], F32)
    mixT, mixT_b = T("mixT", [128, 4, TQ], BF16)
    ovA, sqr_b = T("ovA", [128, 4 * TQ], F32)
    ovB, xqn_b = T("ovB", [128, 4 * TQ], F32)
    macc = [(ovA[:, c * TQ:(c + 1) * TQ], P.buf(f"macc{c}")) for c in range(4)]
    ytmps = [(uT[:].rearrange("p g t -> p (g t)")[:, k * D:(k + 1) * D], P.buf(f"ytmp{k}")) for k in range(2)]
    mT = ovB[:].bitcast(BF16).rearrange("p (k t) -> p k t", k=8)
    mT_b = P.buf("mT")
    qbf, qbf_b = T("qbf", [128, 4 * TQ], BF16)
    st32 = [T(f"st32_{i}", [128, 32], F32) for i in range(3)]
    QT, QT_b = T("QT", [128, 4, TQ], BF16)
    xqT, xqT_b = T("xqT", [128, 4, TQ], BF16)
    brall = P.sb("brall", [128, 3 * D], F32)
    br = [(brall[:, n * D:(n + 1) * D].bitcast(BF16).rearrange("p (c t) -> p c t", c=4), None) for n in range(3)]
    br_cb = [[P.buf(f"br{n}_{c}") for c in range(4)] for n in range(3)]
    xtx = [(brall[:, n * D:(n + 1) * D], P.buf(f"xtx{n}")) for n in range(3)]
    NPT = 3
    PT = [T(f"PT{i}", [128, 2 * TQ], BF16) for i in range(NPT)]
    rc, rc_b = T("rc", [128, TQ], F32)
    gp, gp_b = T("gp", [128, TQ], F32)
    gtmp = [T(f"gtmp{i}", [128, TQ], F32) for i in range(2)]
    ptmp, ptmp_b = T("ptmp", [128, TQ], F32)
    pexp, pexp_b = T("pexp", [128, 8, NMEM], BF16)
    pTt, pTt_b = T("pTt", [128, 16, 128], BF16)
    xst = [T(f"xst{i}", [128, 8], F32) for i in range(4)]

    for s in (["d_ring%d" % i for i in range(NRING)] + ["d_xt%d" % i for i in range(NXT + 3)] +
              ["d_rope0", "d_rope1", "d_edge0", "d_edge1"] + ["d_c%d" % i for i in range(9)]):
        P.new_sem(s)

    def v_ts(out, in0, s1, s2, op0, op1=None, reads=(), writes=(), eng="dve"):
        h = nc.vector if eng == "dve" else nc.gpsimd
        if op1 is None:
            P.op(eng, lambda: h.tensor_scalar(out=out, in0=in0, scalar1=s1, scalar2=None, op0=op0), reads, writes)
        else:
            P.op(eng, lambda: h.tensor_scalar(out=out, in0=in0, scalar1=s1, scalar2=s2, op0=op0, op1=op1), reads, writes)

    def v_tt(out, in0, in1, op, reads=(), writes=(), eng="dve"):
        h = nc.vector if eng == "dve" else nc.gpsimd
        P.op(eng, lambda: h.tensor_tensor(out=out, in0=in0, in1=in1, op=op), reads, writes)

    def v_stt(out, in0, scalar, in1, op0, op1, reads=(), writes=()):
        P.op("dve", lambda: nc.vector.scalar_tensor_tensor(out=out, in0=in0, scalar=scalar, in1=in1, op0=op0, op1=op1),
             reads, writes)

    def v_copy(out, in_, reads=(), writes=(), eng="dve"):
        if eng == "dve":
            P.op("dve", lambda: nc.vector.tensor_copy(out=out, in_=in_), reads, writes)
        elif eng == "act":
            P.op("act", lambda: nc.scalar.copy(out=out, in_=in_), reads, writes)
        else:
            P.op("pool", lambda: nc.gpsimd.tensor_copy(out=out, in_=in_), reads, writes)

    def mm(out, lhsT, rhs, start, stop, reads=(), writes=(), signal=None):
        if signal is None:
            signal = stop
        P.op("pe", lambda: nc.tensor.matmul(out, lhsT=lhsT, rhs=rhs, start=start, stop=stop), reads, writes, signal=signal)

    def tr(out, in_, reads=(), writes=(), signal=True):
        P.op("pe", lambda: nc.tensor.transpose(out=out, in_=in_, identity=ident[0:in_.shape[0], 0:in_.shape[0]]),
             list(reads) + [ident_b], writes, signal=signal)

    def rsqrt_small(ss_ap, v_ap, r_ap, mh_ap, scale, eps, ss_b, v_b, r_b):
        v_ts(v_ap, ss_ap, scale, eps, ALU.mult, ALU.add, reads=[ss_b], writes=[v_b], eng="pool")
        P.op("pool", lambda: nc.gpsimd.tensor_tensor(out=r_ap, in0=v_ap, in1=mh_ap, op=ALU.pow),
             reads=[v_b, mhalf_b], writes=[r_b])

    ring_state = {"n": 0}

    def ring_load(unit, ncols=4096):
        i = ring_state["n"] % NRING
        ring_state["n"] += 1
        t, b = ring[i]
        P.dma("sp", t[:, 0:ncols], scr_d[unit][:, 0:ncols], f"d_ring{i}", reads=[scr_b[unit]], writes=[b])
        return t, b

    cidx = [0]

    def cload(t_ap, d_ap, b):
        s = f"d_c{cidx[0] % 9}"
        cidx[0] += 1
        P.dma("sp", t_ap, d_ap, s, writes=[b])

    cload(ident_f[:], ident_d, ident_f_b)
    cload(lnpre[:], lnpre_d, lnpre_b)
    cload(lnmem[:], lnmem_d, lnmem_b)
    cload(bmh[:], bmerge_d, bmh_b)
    cload(lnpost[:], lnpost_d, lnpost_b)
    cload(geo[:], geo_d, geo_b)
    cload(pscale[:], pscale_d, pscale_b)
    v_copy(ident[:], ident_f[:], reads=[ident_f_b], writes=[ident_b])
    P.op("pool", lambda: nc.gpsimd.memset(mhalf[:], -0.5), writes=[mhalf_b])
    P.op("pool", lambda: nc.gpsimd.memset(VA[:, :, 64:128], 1.0), writes=[VA_b])
    v_ts(bmh[:], bmh[:], 0.5, None, ALU.mult, reads=[bmh_b], writes=[bmh_b])
    P.op("dve", lambda: nc.vector.tensor_reduce(out=mxa[:, 0:1], in_=geo[:, 0:64], axis=AX.X, op=ALU.max,
                                                apply_absolute_value=True), reads=[geo_b], writes=[mxa_b])
    P.op("dve", lambda: nc.vector.tensor_reduce(out=mxa[:, 1:2], in_=geo[:, 64:128], axis=AX.X, op=ALU.max,
                                                apply_absolute_value=True), reads=[geo_b, mxa_b], writes=[mxa_b])
    v_stt(negM[:], mxa[:, 0:1], -8.0, mxa[:, 1:2], ALU.mult, ALU.mult, reads=[mxa_b], writes=[negM_b])

    stage_i = [0]

    def stage_load(loads):
        i = stage_i[0] % NXT
        stage_i[0] += 1
        t, b = xt[i]
        for k, (lo, hi, plo, phi, d_ap) in enumerate(loads):
            P.dma("sp", t[plo:phi, lo:hi], d_ap, f"d_xt{i}", writes=[b], join=(k > 0))
        return t, b, i

    cv = [0]

    def convert_unit(unit, pieces, ncols_total, dest=None):
        if dest is None:
            ri = ring_state["n"] % NRING
            ring_state["n"] += 1
            rt, rb = ring[ri]
        else:
            rt, rb = dest
        for (loads, convs) in pieces:
            t, b, _ = stage_load(loads)
            for (lo, hi, sc, dlo) in convs:
                dst = rt[:, dlo:dlo + (hi - lo)] if dest is None else rt[dlo]
                eng = "dve" if cv[0] % 2 == 0 else "act"
                cv[0] += 1
                rd = [b] + ([lnpre_b, lnmem_b] if sc is not None else [])
                if sc is None:
                    v_copy(dst, t[:, lo:hi], reads=rd, writes=[rb], eng=eng)
                elif eng == "dve":
                    v_ts(dst, t[:, lo:hi], sc, None, ALU.mult, reads=rd, writes=[rb])
                else:
                    P.op("act", lambda dst=dst, src=t[:, lo:hi], sc=sc: nc.scalar.activation(out=dst, in_=src, func=ACT.Identity, scale=sc),
                         reads=rd, writes=[rb])
        if dest is None:
            P.dma("act", scr_d[unit][:, 0:ncols_total], rt[:, 0:ncols_total], f"d_ring{ri}", reads=[rb], writes=[scr_b[unit]])

    def win_cols(unit):
        if unit == U_POOL_IN:
            return [(0 + c * 128, 128) for c in range(4)]
        if unit == U_POOL_GATE:
            return [(512 + c * 128, 128) for c in range(4)]
        if unit == U_Q:
            return [[(1024 + b * 64, 64), (1024 + (b + 4) * 64, 64)] for b in range(4)]
        if unit == U_ATTN_GATE:
            return [[(1792 + b * 64, 64), (1792 + (b + 4) * 64, 64)] for b in range(4)]
        if unit == U_XQ:
            return [(2304 + c * 128, 128) for c in range(4)]
        if unit == U_XGATE:
            return [(2816 + c * 128, 128) for c in range(4)]
        n, half = divmod(unit - U_MERGE0, 2)
        return [(3328 + n * 1024 + half * 512 + c * 128, 128) for c in range(4)]

    w_in_v = w_in_d.rearrange("(kc p) c -> p kc c", p=128)
    w_mkv_v = w_mkv_d.rearrange("(kc p) c -> p kc c", p=128)
    w_out_v = w_out_d.rearrange("(kc p) c -> p kc c", p=128)

    def conv_win(unit):
        cols = win_cols(unit)
        pieces = []
        for kp in range(4):
            loads, convs = [], []
            for j in range(2):
                kc = kp * 2 + j
                if isinstance(cols[0], list):
                    for c in range(4):
                        off = 0
                        for (c0, ln) in cols[c]:
                            loads.append((j * 512 + c * 128 + off, j * 512 + c * 128 + off + ln, 0, 128, w_in_v[:, kc, c0:c0 + ln]))
                            off += ln
                else:
                    loads.append((j * 512, j * 512 + 512, 0, 128, w_in_v[:, kc, cols[0][0]:cols[0][0] + 512]))
                convs.append((j * 512, j * 512 + 512, lnpre[:, kc:kc + 1], kc * 512))
            pieces.append((loads, convs))
        convert_unit(unit, pieces, 4096)

    def conv_plain(unit, view, half, scale_t):
        pieces = []
        for kp in range(4):
            loads = [(j * 512, j * 512 + 512, 0, 128, view[:, kp * 2 + j, half * 512:half * 512 + 512]) for j in range(2)]
            convs = [(j * 512, j * 512 + 512, (scale_t[:, kp * 2 + j:kp * 2 + j + 1] if scale_t is not None else None),
                      (kp * 2 + j) * 512) for j in range(2)]
            pieces.append((loads, convs))
        convert_unit(unit, pieces, 4096)

    pieces = []
    for kp in range(2):
        loads = [(j * 256, j * 256 + 256, 0, 128, w_in_v[:, kp * 4 + j, 1536:1792]) for j in range(4)]
        convs = [(j * 256, j * 256 + 256, lnpre[:, kp * 4 + j:kp * 4 + j + 1], kp * 4 + j) for j in range(4)]
        pieces.append((loads, convs))
    convert_unit(None, pieces, 0, dest=([Wkv[:, kc, :] for kc in range(8)], Wkv_b))
    for half in range(2):
        conv_plain(U_MK + half, w_mkv_v, half, lnmem)
    for unit in range(0, 12):
        conv_win(unit)
    for n in range(3):
        for half in range(2):
            pieces = []
            for kp in range(2):
                loads, convs = [], []
                for j in range(2):
                    kc = kp * 2 + j
                    if n == 1:
                        loads.append((j * 512, j * 512 + 512, 0, 64, w_br_d[n, kc * 64:kc * 64 + 64, half * 512:half * 512 + 512]))
                        loads.append((j * 512, j * 512 + 512, 64, 128, w_br_d[n, (kc + 4) * 64:(kc + 4) * 64 + 64, half * 512:half * 512 + 512]))
                    else:
                        loads.append((j * 512, j * 512 + 512, 0, 128, w_br_d[n, kc * 128:kc * 128 + 128, half * 512:half * 512 + 512]))
                    convs.append((j * 512, j * 512 + 512, None, kc * 512))
                pieces.append((loads, convs))
            convert_unit(U_BR0 + n * 2 + half, pieces, 2048)
    for half in range(2):
        conv_plain(U_OUT0 + half, w_out_v, half, None)
    t, b, si = stage_load([])
    for g in range(4):
        P.dma("sp", t[:, g * 128:(g + 1) * 128], w_pool_d[g], f"d_xt{si}", writes=[b], join=(g > 0))
    v_tt(Wpool[:].rearrange("p g d -> p (g d)"), t[:, 0:512], pscale[:], ALU.mult, reads=[b, pscale_b], writes=[Wpool_b])

    prep_n = [0]
    xt_n = [0]

    def next_xt(kv=False):
        n_ = NXT + 3 if kv else NXT
        i = xt_n[0] % n_
        xt_n[0] += 1
        if i < NXT:
            return xt[i][0][:], xt[i][1], i
        return xtx[i - NXT][0], xtx[i - NXT][1], i

    class Prep:
        def __init__(self, x_rows_ap, nsub, rope_ap=None, gsel=None, edge_ap=None, halo_aps=None, kv=False):
            self.kv = kv
            self.sl = prep_n[0] % 2
            prep_n[0] += 1
            self.x, self.nsub, self.rope_ap, self.gsel, self.edge_ap, self.halo_aps = x_rows_ap, nsub, rope_ap, gsel, edge_ap, halo_aps
            self.xl = {}
            self.xh = None
            self.bki = 0

        def _bank(self, banks):
            if banks is None:
                return next_bank()
            b_ = banks[self.bki % len(banks)]
            self.bki += 1
            return b_

        def start(self):
            self.loads()
            self.squares()

        def loads(self):
            sl = self.sl
            if self.rope_ap is not None:
                rt_, rt_b = ropes[sl]
                P.dma("sp", rt_[:], self.rope_ap.rearrange("(s p) c -> p s c", p=128), f"d_rope{sl}", writes=[rt_b])
            if self.edge_ap is not None:
                P.dma("sp", edges[sl][0][:], self.edge_ap, f"d_edge{sl}", writes=[edges[sl][1]])
            for s in range(self.nsub):
                t, b, i = next_xt(self.kv)
                P.dma("sp", t, self.x[s * 128:(s + 1) * 128, :], f"d_xt{i}", writes=[b])
                self.xl[s] = (t, b)

        def squares(self):
            sl = self.sl
            (ss, ss_b), (vv, vv_b), (rr_, rr_b) = stxs[sl]
            hT, hT_b = hTs[sl]
            for s in range(self.nsub):
                t, b = self.xl[s]
                P.op("act", lambda t=t, s=s: nc.scalar.activation(out=hT[:, :, s * 128:(s + 1) * 128], in_=t.rearrange("p (k c) -> p k c", k=8),
                                                                  func=ACT.Square, accum_out=ss[:, s:s + 1]),
                     reads=[b], writes=[hT_b, ss_b])
            if self.kv:
                ns = self.nsub
                rsqrt_small(ss[:, 0:ns], vv[:, 0:ns], rr_[:, 0:ns], mhalf[:, 0:ns], 1.0 / D, EPS, ss_b, vv_b, rr_b)
            else:
                for s in range(self.nsub):
                    rsqrt_small(ss[:, s:s + 1], vv[:, s:s + 1], rr_[:, s:s + 1], mhalf[:, 0:1], 1.0 / D, EPS, ss_b, vv_b, rr_b)

        def scale(self, subs):
            (ss, ss_b), (vv, vv_b), (rr_, rr_b) = stxs[self.sl]
            for s in subs:
                t, b = self.xl[s]
                xb, xb_b = xbf[s % 2]
                v_ts(xb[:], t, rr_[:, s:s + 1], None, ALU.mult, reads=[b, rr_b], writes=[xb_b])

        def transposes(self, subs, banks=None, ceng=None):
            hT, hT_b = hTs[self.sl]
            for s in subs:
                xb, xb_b = xbf[s % 2]
                bk = self._bank(banks)
                pv = bank_bf(bk)
                for kc in range(8):
                    tr(pv[:, kc * 128:(kc + 1) * 128], xb[:, kc * 128:(kc + 1) * 128], reads=[xb_b], writes=[bank_b[bk]],
                       signal=(kc == 7))
                v_copy(hT[:, :, s * 128:(s + 1) * 128], pv.rearrange("p (k t) -> p k t", k=8),
                       reads=[bank_b[bk]], writes=[hT_b], eng=(ceng or ("act" if s % 2 else "dve")))

        def tables(self):
            if self.rope_ap is None:
                return
            sl, gsel = self.sl, self.gsel
            rt_, rt_b = ropes[sl]
            tb, tb_b = tabss[sl]
            ge = geo[:, gsel * 64:gsel * 64 + 32].unsqueeze(1).broadcast_to([128, 4, 32])
            go = geo[:, gsel * 64 + 32:gsel * 64 + 64].unsqueeze(1).broadcast_to([128, 4, 32])
            cs, sn = rt_[:, :, 0:32], rt_[:, :, 32:64]
            for k, (a_, g_) in enumerate(((cs, ge), (sn, go), (sn, ge), (cs, go))):
                v_tt(tb[:, k, :, :], a_, g_, ALU.mult, reads=[rt_b, geo_b], writes=[tb_b], eng="pool")

        def halo_load(self):
            if self.halo_aps is None:
                return
            xh_t, xh_b, xh_i = next_xt()
            P.dma("sp", xh_t[0:8, :], self.halo_aps[0], f"d_xt{xh_i}", writes=[xh_b])
            P.dma("sp", xh_t[8:16, :], self.halo_aps[1], f"d_xt{xh_i}", writes=[xh_b], join=True)
            self.xh = (xh_t[0:16, :], xh_b)

        def halo_compute(self):
            if self.halo_aps is None:
                return
            if self.xh is None:
                self.halo_load()
            xh, xh_b = self.xh
            (hs, hs_b), (hv, hv_b), (hr, hr_b) = sth
            xhbf, xhbf_b = xbf[0][0][0:16, :], xbf[0][1]
            P.op("act", lambda: nc.scalar.activation(out=xhbf, in_=xh, func=ACT.Square, accum_out=hs[:]),
                 reads=[xh_b], writes=[xhbf_b, hs_b])
            rsqrt_small(hs[:], hv[:], hr[:], mhalf[0:16, 0:1], 1.0 / D, EPS, hs_b, hv_b, hr_b)
            v_ts(xhbf, xh, hr[:], None, ALU.mult, reads=[xh_b, hr_b], writes=[xhbf_b])

        def halo_transposes(self, banks=None):
            if self.halo_aps is None:
                return
            hTh, hTh_b = hThs[self.sl]
            xhbf, xhbf_b = xbf[0][0][0:16, :], xbf[0][1]
            bk = self._bank(banks)
            pv = bank_bf(bk)
            for kc in range(8):
                tr(pv[:, kc * 16:(kc + 1) * 16], xhbf[:, kc * 128:(kc + 1) * 128], reads=[xhbf_b], writes=[bank_b[bk]], signal=(kc == 7))
            v_copy(hTh[:].rearrange("p k t -> p (k t)"), pv[:, 0:128], reads=[bank_b[bk]], writes=[hTh_b])

        def stage_end(self, k):
            if k == 0:
                self.loads()
            elif k == 1:
                self.squares()
                self.scale([0, 1])
                self.halo_load()
            elif k == 2:
                self.scale([2, 3])
                self.tables()

        def stage_mid(self, k):
            if k == 2:
                self.transposes([0, 1], banks=[6, 7], ceng="dve")
            elif k == 3:
                self.transposes([2, 3], banks=[6, 7], ceng="dve")
                self.halo_compute()
                self.halo_transposes(banks=[6])

        def all(self):
            self.start()
            for s in range(self.nsub):
                self.scale([s])
                self.transposes([s])
            self.tables()
            self.halo_compute()
            self.halo_transposes()
            return self.sl

    def prep(x_rows_ap, nsub, **kw):
        return Prep(x_rows_ap, nsub, **kw).all()

    def norm_rope(src4, src_bufs, H, sl, part=0):
        n = 4 * H * 64
        NH = 4 * H
        (ss, ss_b), (vv, vv_b), (rs_, rs_b) = st32
        tb, tb_b = tabss[sl]
        if part != 2:
            sq3 = ovA[:, 0:n].rearrange("p (s c) -> p s c", s=4)
            src3 = src4.rearrange("p s h d -> p s (h d)")
            P.op("act", lambda: nc.scalar.activation(out=sq3, in_=src3, func=ACT.Square), reads=src_bufs, writes=[sqr_b])
            P.op("dve", lambda: nc.vector.tensor_reduce(out=ss[:, 0:NH], in_=ovA[:, 0:n].rearrange("p (a d) -> p a d", d=64), axis=AX.X, op=ALU.add),
                 reads=[sqr_b], writes=[ss_b])
            rsqrt_small(ss[:, 0:NH], vv[:, 0:NH], rs_[:, 0:NH], mhalf[:, 0:NH], 1.0 / 64, EPS, ss_b, vv_b, rs_b)
        if part == 1:
            return
        xn4 = ovB[:, 0:n].rearrange("p (s h d) -> p s h d", s=4, h=H)
        rsb = rs_[:, 0:NH].rearrange("p (s h) -> p s h", s=4).unsqueeze(3).broadcast_to([128, 4, H, 64])
        v_tt(xn4, src4, rsb, ALU.mult, reads=list(src_bufs) + [rs_b], writes=[xqn_b])
        xv = ovB[:, 0:n].rearrange("p (s h i two) -> p s h i two", s=4, h=H, two=2)
        ov = qbf[:, 0:n].rearrange("p (s h i two) -> p s h i two", s=4, h=H, two=2)
        x0, x1 = xv[:, :, :, :, 0], xv[:, :, :, :, 1]
        o0, o1 = ov[:, :, :, :, 0], ov[:, :, :, :, 1]
        hn = n // 2
        a = ovA[:, 0:hn].rearrange("p (s h i) -> p s h i", s=4, h=H)
        bb = ovA[:, hn:2 * hn].rearrange("p (s h i) -> p s h i", s=4, h=H)
        Tb = [tb[:, k, :, :].unsqueeze(2).broadcast_to([128, 4, H, 32]) for k in range(4)]
        rd = [xqn_b, tb_b]
        v_tt(a, x0, Tb[0], ALU.mult, reads=rd + [sqr_b], writes=[sqr_b])
        v_tt(bb, x1, Tb[1], ALU.mult, reads=rd, writes=[sqr_b])
        v_tt(o0, a, bb, ALU.subtract, reads=[sqr_b], writes=[qbf_b])
        v_tt(a, x0, Tb[2], ALU.mult, reads=rd, writes=[sqr_b])
        v_tt(bb, x1, Tb[3], ALU.mult, reads=rd, writes=[sqr_b])
        v_tt(o1, a, bb, ALU.add, reads=[sqr_b], writes=[qbf_b])

    def early_overlay():
        P.inherit([sqr_b], [m[1] for m in macc])
        P.inherit([xqn_b], [mT_b])
        P.inherit([uT_b], [y[1] for y in ytmps])

    def mem_phase(mem_ap):
        sl = prep(mem_ap, 2)
        hT, hT_b = hTs[sl]
        wk, wk_b = ring_load(U_MK)
        wv, wv_b = ring_load(U_MV)
        wkv_ = wk[:].rearrange("p (k c) -> p k c", k=8)
        wvv_ = wv[:].rearrange("p (k c) -> p k c", k=8)
        for c in range(4):
            bk = next_bank()
            for kc in range(8):
                mm(bank(bk)[:, 0:NMEM], wkv_[:, kc, c * 128:(c + 1) * 128], hT[:, kc, 0:NMEM], kc == 0, kc == 7,
                   reads=[wk_b, hT_b], writes=[bank_b[bk]])
            v_copy(memKT[:, c, :], bank(bk)[:, 0:NMEM], reads=[bank_b[bk]], writes=[memKT_b], eng=("act" if c % 2 else "dve"))
        for mc in range(2):
            bk = next_bank()
            for kc in range(8):
                mm(bank(bk), hT[:, kc, mc * 128:(mc + 1) * 128], wvv_[:, kc, :], kc == 0, kc == 7,
                   reads=[wv_b, hT_b], writes=[bank_b[bk]])
            v_copy(memV[:, mc, :], bank(bk), reads=[bank_b[bk]], writes=[memV_b], eng=("act" if mc % 2 else "dve"))

    def kv_phase(xkv_ap, rope_ap, Lkv, after_last=None):
        nt = Lkv // TQ
        early_overlay()
        for n in range(3):
            P.inherit([xtx[n][1]], br_cb[n])
        sl = prep(xkv_ap[0:TQ, :], 4, rope_ap=rope_ap[0:TQ, :], gsel=1, kv=True)
        for ti in range(nt):
            t0 = ti * TQ
            hT, hT_b = hTs[sl]
            bk = next_pair()
            for s in range(4):
                o = ps[:, bk * 512 + s * 256: bk * 512 + (s + 1) * 256]
                bb = bank_b[bk + s // 2]
                for kc in range(8):
                    mm(o, hT[:, kc, s * 128:(s + 1) * 128], Wkv[:, kc, :], kc == 0, kc == 7, reads=[hT_b, Wkv_b], writes=[bb])
            cur_sl = sl
            if ti + 1 < nt:
                sl = prep(xkv_ap[t0 + TQ:t0 + 2 * TQ, :], 4, rope_ap=rope_ap[t0 + TQ:t0 + 2 * TQ, :], gsel=1, kv=True)
            elif after_last is not None:
                sl = after_last()
            kvv = ps[:, bk * 512:(bk + 2) * 512].rearrange("p (s c) -> p s c", s=4)
            bks = [bank_b[bk], bank_b[bk + 1]]
            v_copy(VA[:, ti * 4:(ti + 1) * 4, 0:64], kvv[:, :, 128:192], reads=bks, writes=[VA_b])
            v_copy(VA[:, ti * 4:(ti + 1) * 4, 128:192], kvv[:, :, 192:256], reads=bks, writes=[VA_b])
            norm_rope(kvv[:, :, 0:128].rearrange("p s (h d) -> p s h d", h=2), bks, 2, cur_sl)
            bk2 = next_bank()
            pv = bank_bf(bk2)
            for s in range(4):
                tr(pv[:, s * 128:(s + 1) * 128], qbf[:, s * 128:(s + 1) * 128], reads=[qbf_b], writes=[bank_b[bk2]], signal=(s == 3))
            v_copy(KT[:, t0:t0 + TQ], pv[:, 0:512], reads=[bank_b[bk2]], writes=[KT_b], eng="act")
        return sl

    def prep_q(seg, ti):
        t0 = ti * TQ
        xq_ap = seg["xq"]
        return Prep(xq_ap[t0 + 8:t0 + 8 + TQ, :], 4, rope_ap=seg["rope_q"][t0:t0 + TQ, :], gsel=0, edge_ap=seg["edge"][ti],
                    halo_aps=(xq_ap[t0:t0 + 8, :], xq_ap[t0 + 8 + TQ:t0 + 16 + TQ, :]))

    def q_tile(seg, ti, sl, next_prep):
        Lkv = seg["Lkv"]
        t0 = ti * TQ
        hT, hT_b = hTs[sl]
        hTh, hTh_b = hThs[sl]
        edge_t, edge_t_b = edges[sl]
        early_overlay()
        if ti == 0:
            for n in range(3):
                P.inherit(br_cb[n], [xtx[n][1]])

        def fm_unit(unit, evac):
            w, w_b = ring_load(unit)
            wv_ = w[:].rearrange("p (k c) -> p k c", k=8)
            for c in range(4):
                bk = next_bank()
                for kc in range(8):
                    mm(bank(bk), wv_[:, kc, c * 128:(c + 1) * 128], hT[:, kc, :], kc == 0, kc == 7, reads=[w_b, hT_b], writes=[bank_b[bk]])
                evac(c, bk)
            return w, w_b, wv_

        gi = [0]

        def silu_gate_evac(n):
            def ev(c, bk):
                g, g_b = gtmp[gi[0] % 2]
                gi[0] += 1
                P.op("act", lambda: nc.scalar.activation(out=g[:], in_=bank(bk), func=ACT.Tanh, scale=0.5), reads=[bank_b[bk]], writes=[g_b])
                v_stt(br[n][0][:, c, :], g[:], 1.0, bank(bk), ALU.add, ALU.mult, reads=[g_b, bank_b[bk]], writes=[br_cb[n][c]])
            return ev

        w, w_b = ring_load(U_Q)
        wv_ = w[:].rearrange("p (k c) -> p k c", k=8)
        qb = next_quad()
        for s in range(4):
            for kc in range(8):
                mm(bank(qb + s), hT[:, kc, s * 128:(s + 1) * 128], wv_[:, kc, :], kc == 0, kc == 7, reads=[w_b, hT_b], writes=[bank_b[qb + s]])
        qbks = [bank_b[qb + s] for s in range(4)]
        q4 = bank(qb, 4).rearrange("p (s h d) -> p s h d", s=4, h=8)

        def ev_xq(c, bk):
            v_copy(xqT[:, c, :], bank(bk), reads=[bank_b[bk]], writes=[xqT_b], eng=("act" if c % 2 else "dve"))
        fm_unit(U_XQ, ev_xq)
        norm_rope(q4, qbks, 8, sl, part=1)

        nmax, nmax_b = xst[0]
        nb_, nb_b = xst[1]
        sume, sume_b = xst[2]
        rs_, rs_b = xst[3]
        xscale = 128 ** -0.5

        def cross_stage1(wave):
            bk = next_quad(avoid=(qb if wave == 0 else None))
            for hh in range(2):
                h = wave * 2 + hh
                for s in range(4):
                    o = ps[:, (bk + hh * 2) * 512 + s * 256: (bk + hh * 2) * 512 + (s + 1) * 256]
                    mm(o, xqT[:, h, s * 128:(s + 1) * 128], memKT[:, h, :], True, True, reads=[xqT_b, memKT_b],
                       writes=[bank_b[bk + hh * 2 + s // 2]])
            sv = bank(bk, 4).rearrange("p (a m) -> p a m", a=8)
            bks = [bank_b[bk + i] for i in range(4)]
            P.op("dve", lambda: nc.vector.tensor_reduce(out=nmax[:], in_=sv, axis=AX.X, op=ALU.max, negate=True), reads=bks, writes=[nmax_b])
            v_ts(nb_[:], nmax[:], xscale, None, ALU.mult, reads=[nmax_b], writes=[nb_b])
            for a in range(8):
                P.op("act", lambda a=a: nc.scalar.activation(out=pexp[:, a, :], in_=sv[:, a, :], func=ACT.Exp, bias=nb_[:, a:a + 1],
                                                              scale=xscale, accum_out=sume[:, a:a + 1]),
                     reads=bks + [nb_b], writes=[pexp_b, sume_b])

        def cross_stage1b():
            P.op("dve", lambda: nc.vector.reciprocal(out=rs_[:], in_=sume[:]), reads=[sume_b], writes=[rs_b])
            v_tt(pexp[:], pexp[:], rs_[:].unsqueeze(2).broadcast_to([128, 8, NMEM]), ALU.mult, reads=[pexp_b, rs_b], writes=[pexp_b])

        def cross_stage2(wave):
            bk2 = next_pair()
            for hh in range(2):
                pv = bank_bf(bk2 + hh)
                for s in range(4):
                    for mc in range(2):
                        i = s * 2 + mc
                        tr(pv[:, i * 128:(i + 1) * 128], pexp[:, hh * 4 + s, mc * 128:(mc + 1) * 128], reads=[pexp_b],
                           writes=[bank_b[bk2 + hh]], signal=(i == 7))
                v_copy(pTt[:, hh * 8:(hh + 1) * 8, :].rearrange("p a t -> p (a t)"), pv, reads=[bank_b[bk2 + hh]], writes=[pTt_b], eng="act")
            bk3 = next_pair()
            for hh in range(2):
                h = wave * 2 + hh
                for s in range(4):
                    for mc in range(2):
                        mm(bank(bk3 + hh)[:, s * 128:(s + 1) * 128], memV[:, mc, h * 128:(h + 1) * 128], pTt[:, hh * 8 + s * 2 + mc, :],
                           mc == 0, mc == 1, reads=[memV_b, pTt_b], writes=[bank_b[bk3 + hh]], signal=(mc == 1 and s == 3))
                v_tt(br[2][0][:, h, :], bank(bk3 + hh), br[2][0][:, h, :], ALU.mult, reads=[bank_b[bk3 + hh], br_cb[2][h]], writes=[br_cb[2][h]])

        cross_stage1(0)
        norm_rope(q4, qbks, 8, sl, part=2)
        cross_stage1b()
        fm_unit(U_XGATE, silu_gate_evac(2))
        cross_stage2(0)
        cross_stage1(1)

        def ev_pool_in(c, bk):
            v_copy(uT[:, c, 8:8 + TQ], bank(bk), reads=[bank_b[bk]], writes=[uT_b], eng=("act" if c % 2 else "dve"))
        w, w_b, wv_ = fm_unit(U_POOL_IN, ev_pool_in)
        bk = next_bank()
        for c in range(4):
            for kc in range(8):
                mm(bank(bk)[:, c * 16:(c + 1) * 16], wv_[:, kc, c * 128:(c + 1) * 128], hTh[:, kc, :], kc == 0, kc == 7,
                   reads=[w_b, hTh_b], writes=[bank_b[bk]], signal=(kc == 7 and c == 3))
        hv = bank(bk)[:, 0:64].rearrange("p (c t) -> p c t", c=4)
        v_copy(uT[:, :, 0:8], hv[:, :, 0:8], reads=[bank_b[bk]], writes=[uT_b])
        v_copy(uT[:, :, 8 + TQ:16 + TQ], hv[:, :, 8:16], reads=[bank_b[bk]], writes=[uT_b])

        cross_stage1b()
        for half in range(2):
            bk2 = next_bank()
            pv = bank_bf(bk2)
            for s2 in range(2):
                s = half * 2 + s2
                for b_ in range(4):
                    tr(pv[:, (s2 * 4 + b_) * 128:(s2 * 4 + b_ + 1) * 128], qbf[:, s * 512 + b_ * 128: s * 512 + (b_ + 1) * 128],
                       reads=[qbf_b], writes=[bank_b[bk2]], signal=(s2 == 1 and b_ == 3))
            v_copy(QT[:, :, half * 256:(half + 1) * 256].rearrange("p b (s t) -> p b s t", s=2),
                   pv.rearrange("p (s b t) -> p b s t", s=2, b=4), reads=[bank_b[bk2]], writes=[QT_b], eng="act")

        cross_stage2(1)
        w_pg = ring_load(U_POOL_GATE)

        for g, wdw in enumerate(POOL_W):
            U = uT[:, g, :]
            n2 = TQ + 15
            v_tt(pa[:, 0:n2], U[:, 0:n2], U[:, 1:n2 + 1], ALU.add, reads=[uT_b], writes=[pa_b], eng="pool")
            cur, cur_b, oth, oth_b, n = pa, pa_b, pb, pb_b, n2
            step = 2
            while step < wdw:
                n = n - step
                v_tt(oth[:, 0:n], cur[:, 0:n], cur[:, step:step + n], ALU.add, reads=[cur_b], writes=[oth_b], eng="pool")
                cur, cur_b, oth, oth_b = oth, oth_b, cur, cur_b
                step *= 2
            o = 8 - wdw // 2
            S = cur[:, o:o + TQ]
            v_tt(S[:, 0:8], S[:, 0:8], edge_t[:, g * 16:g * 16 + 8], ALU.mult, reads=[cur_b, edge_t_b], writes=[cur_b])
            v_tt(S[:, TQ - 8:TQ], S[:, TQ - 8:TQ], edge_t[:, g * 16 + 8:g * 16 + 16], ALU.mult, reads=[cur_b, edge_t_b], writes=[cur_b])
            v_stt(mixT[:, g, :], S, 1.0 / wdw, U[:, 8:8 + TQ], ALU.mult, ALU.subtract, reads=[cur_b, uT_b], writes=[mixT_b])

        w_ag = ring_load(U_ATTN_GATE)

        def gate_unit_steps(wb_, n):
            w, w_b = wb_
            wv_ = w[:].rearrange("p (k c) -> p k c", k=8)
            ev = silu_gate_evac(n)
            for c in range(4):
                bk = 6 + (c % 2)
                for kc in range(8):
                    def gstep(c=c, kc=kc, bk=bk):
                        mm(bank(bk), wv_[:, kc, c * 128:(c + 1) * 128], hT[:, kc, :], kc == 0, kc == 7, reads=[w_b, hT_b], writes=[bank_b[bk]])
                        if kc == 7:
                            ev(c, bk)
                    yield (kc == 0, gstep)

        def pool_mm_steps():
            for g in range(4):
                def pstep_(g=g):
                    bk = 6 + (g % 2)
                    mm(bank(bk), Wpool[:, g, :], mixT[:, g, :], True, True, reads=[Wpool_b, mixT_b], writes=[bank_b[bk]])
                    v_tt(br[0][0][:, g, :], bank(bk), br[0][0][:, g, :], ALU.mult, reads=[bank_b[bk], br_cb[0][g]], writes=[br_cb[0][g]])
                yield (True, pstep_)
        pre_steps = list(gate_unit_steps(w_ag, 1)) + list(gate_unit_steps(w_pg, 0)) + list(pool_mm_steps())
        n_gate1 = 32

        P.inherit([m[1] for m in macc], [sqr_b])
        P.inherit([mT_b], [xqn_b])

        in_att = [True]
        wo_pref = {}

        def merge_steps():
            units = [(0, 0), (2, 0), (0, 1), (2, 1), (1, 0), (1, 1)]
            loaded = {}

            def load(k, which):
                if k < len(units) and (k, which) not in loaded:
                    n, half = units[k]
                    if which == "l":
                        loaded[(k, which)] = ring_load(U_MERGE0 + n * 2 + half)
                    else:
                        loaded[(k, which)] = ring_load(U_BR0 + n * 2 + half, 2048)
            st = {}
            for k, (n, half) in enumerate(units):
                for c in range(4):
                    cc = half * 4 + c
                    for kc in range(8):
                        def lstep(k=k, n=n, c=c, cc=cc, kc=kc):
                            if c == 0 and kc == 0:
                                load(k, "l")
                                load(k, "b")
                                load(k + 1, "l")
                            wl, wl_b = loaded[(k, "l")]
                            wlv = wl[:].rearrange("p (k c) -> p k c", k=8)
                            if c == 0 and kc == 0 and k == len(units) - 1:
                                wo_pref[0] = ring_load(U_OUT0)
                            if kc == 0:
                                st["bl"] = 6 if in_att[0] else next_bank()
                            bl = st["bl"]
                            mm(bank(bl), wlv[:, kc, c * 128:(c + 1) * 128], hT[:, kc, :], kc == 0, kc == 7, reads=[wl_b, hT_b], writes=[bank_b[bl]])
                            if kc == 7:
                                g, g_b = gtmp[gi[0] % 2]
                                gi[0] += 1
                                st["g"] = (g, g_b)
                                P.op("act", lambda: nc.scalar.activation(out=g[:], in_=bank(bl), func=ACT.Tanh,
                                                                         bias=bmh[:, n * 8 + cc:n * 8 + cc + 1], scale=0.5),
                                     reads=[bank_b[bl], bmh_b], writes=[g_b])
                                if c == 3:
                                    load(k + 1, "b")
                                    if k == len(units) - 1:
                                        wo_pref[1] = ring_load(U_OUT0 + 1)
                        yield (kc == 0, lstep)
                    for kc in range(4):
                        def pstep(k=k, n=n, c=c, cc=cc, kc=kc):
                            wb, wb_b = loaded[(k, "b")]
                            wbv = wb[:, 0:2048].rearrange("p (k c) -> p k c", k=4)
                            if kc == 0:
                                st["bp"] = 7 if in_att[0] else next_bank()
                            bp = st["bp"]
                            mm(bank(bp), wbv[:, kc, c * 128:(c + 1) * 128], br[n][0][:, kc, :], kc == 0, kc == 3,
                               reads=[wb_b, br_cb[n][kc]], writes=[bank_b[bp]])
                            if kc == 3:
                                g, g_b = st["g"]
                                ma, ma_b = macc[c]
                                if n == 0:
                                    v_stt(ma, g[:], 1.0, bank(bp), ALU.add, ALU.mult, reads=[g_b, bank_b[bp]], writes=[ma_b])
                                elif n == 2:
                                    v_stt(ptmp2[:], g[:], 1.0, bank(bp), ALU.add, ALU.mult, reads=[g_b, bank_b[bp]], writes=[ptmp2_b])
                                    v_tt(mT[:, cc, :], ma, ptmp2[:], ALU.add, reads=[ma_b, ptmp2_b], writes=[mT_b], eng="pool")
                                else:
                                    v_stt(ptmp2[:], g[:], 1.0, bank(bp), ALU.add, ALU.mult, reads=[g_b, bank_b[bp]], writes=[ptmp2_b])
                                    v_tt(mT[:, cc, :], mT[:, cc, :], ptmp2[:], ALU.add, reads=[mT_b, ptmp2_b], writes=[mT_b], eng="pool")
                        yield (False, pstep)

        nj = Lkv // 128
        nsl = None
        ptmp2, ptmp2_b = gp, gp_b
        rcA, rcA_b = ptmp, ptmp_b
        rcB, rcB_b = pb[:, 0:TQ], pb_b
        g2, g2_b = pa[:, 0:TQ], pa_b
        all_steps = pre_steps + list(merge_steps())
        n_att_steps = len(pre_steps) + 1 * 4 * 12
        steps = all_steps[:n_att_steps]
        n_it = 4 * nj
        rate = len(steps) / float(n_it)
        owed = 0.0
        done_steps = 0
        pending_mid = None
        accA, accB = 4, 5
        for b_ in range(4):
            def qk(j):
                sb_ = 0 if j % 2 == 0 else 2
                mm(bank(sb_), KT[0:64, j * 128:(j + 1) * 128], QT[0:64, b_, :], True, True, reads=[KT_b, QT_b], writes=[bank_b[sb_]])
                mm(bank(sb_ + 1), KT[64:128, j * 128:(j + 1) * 128], QT[64:128, b_, :], True, True, reads=[KT_b, QT_b], writes=[bank_b[sb_ + 1]])

            def ex(j):
                sb_ = 0 if j % 2 == 0 else 2
                pt, pt_b = PT[j % NPT]
                P.op("act", lambda: nc.scalar.activation(out=pt[:], in_=bank(sb_, 2), func=ACT.Exp, bias=negM[:], scale=0.125),
                     reads=[bank_b[sb_], bank_b[sb_ + 1], negM_b], writes=[pt_b])

            def pvm(j):
                pt, pt_b = PT[j % NPT]
                mm(bank(accA), VA[:, j, 0:128], pt[:, 0:TQ], j == 0, j == nj - 1, reads=[VA_b, pt_b], writes=[bank_b[accA]])
                mm(bank(accB), VA[:, j, 64:192], pt[:, TQ:2 * TQ], j == 0, j == nj - 1, reads=[VA_b, pt_b], writes=[bank_b[accB]])

            qk(0)
            if nj > 1:
                qk(1)
            for j in range(nj):
                if next_prep is not None and j == nj // 2:
                    pending_mid = b_
                ex(j)
                if j + 2 < nj:
                    qk(j + 2)
                pvm(j)
                owed += rate
                while owed >= 1.0 and done_steps < len(steps):
                    steps[done_steps][1]()
                    done_steps += 1
                    owed -= 1.0
                if pending_mid is not None and (done_steps >= len(steps) or steps[done_steps][0]):
                    next_prep.stage_mid(pending_mid)
                    pending_mid = None
            while done_steps < n_gate1:
                steps[done_steps][1]()
                done_steps += 1
            if pending_mid is not None:
                while done_steps < len(steps) and not steps[done_steps][0]:
                    steps[done_steps][1]()
                    done_steps += 1
                next_prep.stage_mid(pending_mid)
                pending_mid = None
            v_copy(rcA[:], bank(accA), reads=[bank_b[accA]], writes=[rcA_b])
            v_copy(rcB, bank(accB), reads=[bank_b[accB]], writes=[rcB_b])
            P.op("dve", lambda: nc.vector.reciprocal(out=rc[0:64, :], in_=rcA[64:128, :]), reads=[rcA_b], writes=[rc_b])
            P.op("dve", lambda: nc.vector.reciprocal(out=rc[64:128, :], in_=rcB[0:64, :]), reads=[rcB_b], writes=[rc_b])
            v_tt(g2, br[1][0][:, b_, :], rc[:], ALU.mult, reads=[rc_b, br_cb[1][b_]], writes=[g2_b], eng="pool")
            v_tt(br[1][0][0:64, b_, :], rcA[0:64, :], g2[0:64, :], ALU.mult, reads=[rcA_b, g2_b], writes=[br_cb[1][b_]])
            v_tt(br[1][0][64:128, b_, :], rcB[64:128, :], g2[64:128, :], ALU.mult, reads=[rcB_b, g2_b], writes=[br_cb[1][b_]])
            if next_prep is not None:
                next_prep.stage_end(b_)
                nsl = next_prep.sl
        while done_steps < len(steps):
            steps[done_steps][1]()
            done_steps += 1

        xr = []
        for s in range(4):
            t, b, i = next_xt()
            P.dma("sp", t, seg["xq"][t0 + 8 + s * 128:t0 + 8 + (s + 1) * 128, :], f"d_xt{i}", writes=[b])
            xr.append((t, b, i))
        in_att[0] = False
        for _, step in all_steps[n_att_steps:]:
            step()

        P.inherit([y[1] for y in ytmps], [uT_b])
        wo = [wo_pref[0], wo_pref[1]]
        (ss, ss_b), (vv, vv_b), (rr_, rr_b) = stxs[sl]
        rr[0] = 0
        for half in range(2):
            w, w_b = wo[half]
            wv_ = w[:].rearrange("p (k c) -> p k c", k=8)
            for s in range(4):
                for kc in range(8):
                    mm(bank(2 * s + half), mT[:, kc, s * 128:(s + 1) * 128], wv_[:, kc, :], kc == 0, kc == 7, reads=[w_b, mT_b],
                       writes=[bank_b[2 * s + half]])
        for s in range(4):
            bk = 2 * s
            bks = [bank_b[bk], bank_b[bk + 1]]
            z = bank(bk, 2)
            yt, yt_b = ytmps[s % 2]
            P.op("act", lambda z=z, s=s, yt=yt: nc.scalar.activation(out=yt, in_=z, func=ACT.Square, accum_out=ss[:, s:s + 1]),
                 reads=bks, writes=[yt_b, ss_b])
            rsqrt_small(ss[:, s:s + 1], vv[:, s:s + 1], rr_[:, s:s + 1], mhalf[:, 0:1], 1.0 / D, 16.0 * EPS, ss_b, vv_b, rr_b)
            v_stt(yt, z, rr_[:, s:s + 1], lnpost[:], ALU.mult, ALU.mult, reads=bks + [rr_b, lnpost_b], writes=[yt_b])
            t, b, i = xr[s]
            v_tt(t, t, yt, ALU.add, reads=[b, yt_b], writes=[b], eng="pool")
        for s in range(4):
            t, b, i = xr[s]
            P.dma("act", seg["y"][t0 + s * 128:t0 + (s + 1) * 128, :], t, f"d_xt{i}", reads=[b])
        return nsl

    segs = []
    for i in range(NS):
        segs.append(dict(xq=xs_d[i], xkv=xs_d[i][8:8 + LS, :], mem=mem_d[i], rope_q=rope_s_d, rope_k=rope_s_d,
                         edge=edge_s_d, y=ys_d[i], Lq=LS, Lkv=LS))
    if LQP > 0:
        segs.append(dict(xq=xqp_d, xkv=xp_d, mem=mem_d[NS], rope_q=rope_qp_d, rope_k=rope_p_d,
                         edge=edge_qp_d, y=yp_d, Lq=LQP, Lkv=LKP))
    for seg in segs:
        mem_phase(seg["mem"])
        sl = kv_phase(seg["xkv"], seg["rope_k"], seg["Lkv"], after_last=lambda seg=seg: prep_q(seg, 0).all())
        nq = seg["Lq"] // TQ
        for ti in range(nq):
            nxt = prep_q(seg, ti + 1) if ti + 1 < nq else None
            sl = q_tile(seg, ti, sl, nxt)

    P.wait_all("sp", [b for (_, b) in xt] + [b for (_, b) in xtx])
    P.wait_all("act", [b for (_, b) in xt] + [b for (_, b) in xtx])
    P.finish()
    return nc, P


def rope_table(L):
    rows = L // 64
    row = np.repeat(np.arange(rows, dtype=np.float32), 64)
    col = np.tile(np.arange(64, dtype=np.float32), rows)
    inv = (np.float32(10000.0) ** (-np.arange(0, 32, 2, dtype=np.float32) / np.float32(32))).astype(np.float32)
    ang = np.concatenate([row[:, None] * inv, col[:, None] * inv], axis=-1).astype(np.float32)
    return np.concatenate([np.cos(ang), np.sin(ang)], axis=-1).astype(np.float32)


def edge_table(L, q0, Lq):
    nt = Lq // TQ
    out = np.ones((nt, 4, 16), np.float32)
    for ti in range(nt):
        for g, w in enumerate(POOL_W):
            for k in range(16):
                t = q0 + ti * TQ + (k if k < 8 else TQ - 16 + k)
                lo = min(max(t - w // 2, 0), L)
                hi = min(max(t + (w - 1 - w // 2) + 1, 0), L)
                out[ti, g, k] = np.float32(w) / np.float32(hi - lo)
    return np.ascontiguousarray(np.broadcast_to(out.reshape(nt, 1, 64), (nt, 128, 64)))


def make_core_inputs(inp, core, cfg):
    NS, LS, LKP, LQP = cfg["NS"], cfg["LS"], cfg["LKP"], cfg["LQP"]
    f = np.float32
    xs_full = inp["x_sample"]
    xp_full = inp["x_prompt"]
    xs = np.zeros((NS, LS + 16, D), f)
    xs[:, 8:8 + LS] = xs_full[core * NS:(core + 1) * NS]
    npq = LKP // LQP if LQP else 1
    pi, qi = divmod(core, npq)
    xp = np.ascontiguousarray(xp_full[pi])
    q0 = qi * LQP
    xpad = np.zeros((LKP + 16, D), f)
    xpad[8:8 + LKP] = xp
    xqp = np.ascontiguousarray(xpad[q0:q0 + LQP + 16])
    mem = np.concatenate([inp["mem_sample"][core * NS:(core + 1) * NS], inp["mem_prompt"][pi:pi + 1]], 0)
    rope_s = rope_table(LS)
    rope_p = rope_table(LKP)
    d = {
        "xs": xs, "xp": xp, "xqp": xqp, "mem": np.ascontiguousarray(mem, f),
        "rope_s": rope_s, "rope_p": rope_p, "rope_qp": np.ascontiguousarray(rope_p[q0:q0 + LQP]),
        "edge_s": edge_table(LS, 0, LS), "edge_qp": edge_table(LKP, q0, LQP),
        "w_in": inp["w_in"], "w_mem_kv": inp["w_mem_kv"], "w_branch": inp["w_branch"], "w_out": inp["w_out"],
        "w_pool": inp["w_pool"],
        "lnpre_l": np.ascontiguousarray(inp["ln_pre"].reshape(8, 128).T),
        "lnmem_l": np.ascontiguousarray(inp["ln_mem"].reshape(8, 128).T),
        "bmerge_l": np.ascontiguousarray(inp["b_merge"].reshape(3, 8, 128).transpose(2, 0, 1).reshape(128, 24)),
        "lnpost_b": np.ascontiguousarray(np.broadcast_to(inp["ln_post"][None, :], (128, D))),
        "geo": np.ascontiguousarray(np.broadcast_to(np.concatenate(
            [inp["q_norm"][0::2], inp["q_norm"][1::2], inp["k_norm"][0::2], inp["k_norm"][1::2]])[None, :], (128, 128))),
        "pscale_b": np.ascontiguousarray(np.broadcast_to(inp["pool_scale"][None, :], (128, 512))),
        "ident": np.eye(128, dtype=f),
    }
    return {k: np.ascontiguousarray(v, dtype=f) for k, v in d.items()}


CFG_FULL = dict(NS=2, LS=4096, LKP=8192, LQP=2048)
_CACHE = {}


def kernel(**inputs):
    inp = {k: np.asarray(v) for k, v in inputs.items()}
    cfg = CFG_FULL
    if "nc" not in _CACHE:
        _CACHE["nc"] = build(cfg)[0]
    nc = _CACHE["nc"]
    in_maps = [make_core_inputs(inp, c, cfg) for c in range(N_CORES)]
    res = run_bass_kernel_spmd(nc, in_maps, core_ids=list(range(N_CORES)))
    y_s = np.concatenate([np.asarray(r["ys"]) for r in res.results], 0).astype(np.float32)
    B, L = inp["x_prompt"].shape[:2]
    y_p = np.empty((B, L, D), np.float32)
    npq = cfg["LKP"] // cfg["LQP"]
    for c in range(N_CORES):
        pi, qi = divmod(c, npq)
        y_p[pi, qi * cfg["LQP"]:(qi + 1) * cfg["LQP"]] = np.asarray(res.results[c]["yp"])
    return (y_p, y_s)
```
